# Optimizing a Trainium2 kernel written in Bass

```python
import math
import jax, jax.numpy as jnp
from jax import lax
import numpy as np

D_MODEL = 1024
BATCH = 4
SEQ = 8192
DEPTH = 4
DEC_BATCH = 16
DEC_SEQ = 16
PAST_LEN = 4096

CHUNK = 64
N_HEADS = 8
HEAD_DIM = 64
V_DIM = 2 * HEAD_DIM
ATTN_W = N_HEADS * V_DIM
CONV_W = 512
CONV_K = 31
POOL_W = 512
POOL_WINDOWS = (2, 4, 8, 16)
POOL_GROUPS = 4
POOL_GW = POOL_W // POOL_GROUPS
POOL_BUF = 15
D_FF = 2816
ROPE_THETA = 10000.0
Q_BLOCK = 128
EPS = 1e-6
N_MOD = 9
IN_W = 2 * CONV_W + POOL_W + 3 * ATTN_W + 3 * D_MODEL
IN_SPLIT = (2 * CONV_W, 2 * CONV_W + POOL_W, 2 * CONV_W + POOL_W + ATTN_W, 2 * CONV_W + POOL_W + 2 * ATTN_W, 2 * CONV_W + POOL_W + 3 * ATTN_W)

kernel_name = 'hybrid_conv_pool_diffattn_stream'


def _rms_norm(x, g):
    xf = x.astype(jnp.float32)
    y = xf * lax.rsqrt(jnp.mean(xf * xf, axis=-1, keepdims=True) + EPS)
    return (y * g.astype(jnp.float32)).astype(x.dtype)


def _layer_norm(x, g, b):
    xf = x.astype(jnp.float32)
    mu = jnp.mean(xf, axis=-1, keepdims=True)
    var = jnp.mean(jnp.square(xf - mu), axis=-1, keepdims=True)
    y = (xf - mu) * lax.rsqrt(var + EPS)
    return (y * g.astype(jnp.float32) + b.astype(jnp.float32)).astype(x.dtype)


def _modulate(h, shift, scale):
    return h * (1.0 + scale[:, None, :]) + shift[:, None, :]


def _rope(x, pos):
    half = HEAD_DIM // 2
    inv = ROPE_THETA ** (-jnp.arange(half, dtype=jnp.float32) / half)
    ang = pos.astype(jnp.float32)[:, None] * inv[None, :]
    cos = jnp.cos(ang)[None, :, None, None, :]
    sin = jnp.sin(ang)[None, :, None, None, :]
    xf = x.astype(jnp.float32)
    x1, x2 = xf[..., :half], xf[..., half:]
    return jnp.concatenate([x1 * cos - x2 * sin, x2 * cos + x1 * sin], axis=-1).astype(x.dtype)


def _swiglu(h, w_in, w_out):
    a, b = jnp.split(h @ w_in, 2, axis=-1)
    return (jax.nn.silu(a) * b) @ w_out


def _conv_branch(u, buf, w_dw, b_dw, ln_g, ln_b, w_proj):
    a, gt = jnp.split(u, 2, axis=-1)
    z = a * jax.nn.sigmoid(gt)
    zc = jnp.concatenate([buf, z], axis=1)
    y = lax.conv_general_dilated(zc, w_dw[:, None, :], (1,), 'VALID',
                                 dimension_numbers=('NWC', 'WIO', 'NWC'),
                                 feature_group_count=CONV_W) + b_dw
    y = jax.nn.silu(_layer_norm(y, ln_g, ln_b))
    return y @ w_proj, zc[:, -(CONV_K - 1):]


def _pool_branch(p, buf, pos, w_grp, scale, w_proj):
    B, T = p.shape[0], p.shape[1]
    pc = jnp.concatenate([buf, p], axis=1)
    cs = jnp.cumsum(pc.astype(jnp.float32), axis=1)
    cs = jnp.concatenate([jnp.zeros_like(cs[:, :1]), cs], axis=1)
    end = cs[:, POOL_BUF + 1:]
    pf = p.astype(jnp.float32)
    outs = []
    for g, w in enumerate(POOL_WINDOWS):
        lo, hi = g * POOL_GW, (g + 1) * POOL_GW
        start = cs[:, POOL_BUF + 1 - w:POOL_BUF + 1 - w + T, lo:hi]
        cnt = jnp.minimum(w, pos + 1).astype(jnp.float32)[None, :, None]
        outs.append((end[..., lo:hi] - start) / cnt - pf[..., lo:hi])
    d = jnp.stack(outs, axis=2).astype(p.dtype)
    m = jnp.einsum('btgi,gio->btgo', d, w_grp).reshape(B, T, POOL_W) * scale
    return m @ w_proj, pc[:, -POOL_BUF:]


def _diff_core(q, k, v, lam, mask):
    s = jnp.einsum('bqhmd,bshmd->bhmqs', q, k).astype(jnp.float32) * (HEAD_DIM ** -0.5)
    if mask is not None:
        s = jnp.where(mask, s, -jnp.inf)
    pr = jax.nn.softmax(s, axis=-1)
    a = pr[:, :, 0] - lam * pr[:, :, 1]
    return jnp.einsum('bhqs,bshv->bqhv', a.astype(v.dtype), v)


def _attn_prompt(q, k, v, lam):
    B, T = q.shape[0], q.shape[1]
    nb = T // Q_BLOCK
    qb = jnp.swapaxes(q.reshape(B, nb, Q_BLOCK, N_HEADS, 2, HEAD_DIM), 0, 1)
    kchunk = jnp.arange(T) // CHUNK

    def block(args):
        qblk, i = args
        qchunk = (i * Q_BLOCK + jnp.arange(Q_BLOCK)) // CHUNK
        mask = kchunk[None, :] <= qchunk[:, None]
        return _diff_core(qblk, k, v, lam, mask)

    o = lax.map(block, (qb, jnp.arange(nb)))
    return jnp.swapaxes(o, 0, 1).reshape(B, T, N_HEADS, V_DIM)


def _layer(x, c, pos, conv_buf, pool_buf, k_past, v_past, p, lam, lam_init):
    B, T = x.shape[0], x.shape[1]
    mod = jax.nn.silu(c) @ p['w_ada'] + p['b_ada']
    sh1, sc1, g1, sh2, sc2, g2, sh3, sc3, g3 = jnp.split(mod, N_MOD, axis=-1)
    h = _modulate(_rms_norm(x, p['g_ffn1']), sh1, sc1)
    x = x + 0.5 * g1[:, None, :] * _swiglu(h, p['w_ffn1_in'], p['w_ffn1_out'])
    h = _modulate(_rms_norm(x, p['g_mix']), sh2, sc2)
    u = h @ p['w_in']
    u_conv, u_pool, u_q, u_k, u_v, u_gate = jnp.split(u, IN_SPLIT, axis=-1)
    y_conv, new_conv = _conv_branch(u_conv, conv_buf, p['w_dw'], p['b_dw'], p['ln_conv_g'], p['ln_conv_b'], p['w_conv_out'])
    y_pool, new_pool = _pool_branch(u_pool, pool_buf, pos, p['w_pool_grp'], p['pool_scale'], p['w_pool_out'])
    q = _rope(_rms_norm(u_q.reshape(B, T, N_HEADS, 2, HEAD_DIM), p['g_q']), pos)
    k = _rope(_rms_norm(u_k.reshape(B, T, N_HEADS, 2, HEAD_DIM), p['g_k']), pos)
    v = u_v.reshape(B, T, N_HEADS, V_DIM)
    if k_past is None:
        o = _attn_prompt(q, k, v, lam)
    else:
        o = _diff_core(q, jnp.concatenate([k_past, k], axis=1), jnp.concatenate([v_past, v], axis=1), lam, None)
    o = _rms_norm(o, p['g_sub']) * (1.0 - lam_init)
    y_attn = o.reshape(B, T, ATTN_W) @ p['w_attn_out']
    gc, gp, ga = jnp.split(jax.nn.sigmoid(u_gate), 3, axis=-1)
    merged = gc * y_conv + gp * y_pool + ga * y_attn
    x = x + g2[:, None, :] * (merged @ p['w_out'])
    h = _modulate(_rms_norm(x, p['g_ffn2']), sh3, sc3)
    x = x + 0.5 * g3[:, None, :] * _swiglu(h, p['w_ffn2_in'], p['w_ffn2_out'])
    return x, k, v, new_conv, new_pool


def setup_inputs(seed: int = 0) -> dict:
    key = jax.random.key(seed)
    ks = iter(jax.random.split(key, 48))
    nrm = lambda shape, s=1.0: jax.random.normal(next(ks), shape, jnp.float32) * s
    gain = lambda shape: 1.0 + 0.02 * jax.random.normal(next(ks), shape, jnp.float32)
    L, D = DEPTH, D_MODEL
    return {
        'x_prompt': nrm((BATCH, SEQ, D)),
        'x_sample': nrm((DEC_BATCH, DEC_SEQ, D)),
        'cache_attn_k': nrm((L, DEC_BATCH, PAST_LEN, N_HEADS, 2, HEAD_DIM)),
        'cache_attn_v': nrm((L, DEC_BATCH, PAST_LEN, N_HEADS, V_DIM)),
        'state_conv': nrm((L, DEC_BATCH, CONV_K - 1, CONV_W), 0.5),
        'state_pool': nrm((L, DEC_BATCH, POOL_BUF, POOL_W)),
        'c_prompt': nrm((BATCH, D)),
        'c_sample': nrm((DEC_BATCH, D)),
        'w_ada': nrm((L, D, N_MOD * D), D ** -0.5),
        'b_ada': nrm((L, N_MOD * D), 0.02),
        'g_ffn1': gain((L, D)),
        'w_ffn1_in': nrm((L, D, 2 * D_FF), D ** -0.5),
        'w_ffn1_out': nrm((L, D_FF, D), D_FF ** -0.5),
        'g_mix': gain((L, D)),
        'w_in': nrm((L, D, IN_W), D ** -0.5),
        'w_dw': nrm((L, CONV_K, CONV_W), CONV_K ** -0.5),
        'b_dw': nrm((L, CONV_W), 0.02),
        'ln_conv_g': gain((L, CONV_W)),
        'ln_conv_b': nrm((L, CONV_W), 0.02),
        'w_conv_out': nrm((L, CONV_W, D), CONV_W ** -0.5),
        'w_pool_grp': nrm((L, POOL_GROUPS, POOL_GW, POOL_GW), POOL_GW ** -0.5),
        'pool_scale': 1.0 + 0.1 * nrm((L, POOL_W)),
        'w_pool_out': nrm((L, POOL_W, D), POOL_W ** -0.5),
        'g_q': gain((L, HEAD_DIM)),
        'g_k': gain((L, HEAD_DIM)),
        'lam_q1': nrm((L, HEAD_DIM), 0.1),
        'lam_k1': nrm((L, HEAD_DIM), 0.1),
        'lam_q2': nrm((L, HEAD_DIM), 0.1),
        'lam_k2': nrm((L, HEAD_DIM), 0.1),
        'g_sub': gain((L, V_DIM)),
        'w_attn_out': nrm((L, ATTN_W, D), ATTN_W ** -0.5),
        'w_out': nrm((L, D, D), D ** -0.5),
        'g_ffn2': gain((L, D)),
        'w_ffn2_in': nrm((L, D, 2 * D_FF), D ** -0.5),
        'w_ffn2_out': nrm((L, D_FF, D), D_FF ** -0.5),
    }


def reference(x_prompt, x_sample, cache_attn_k, cache_attn_v, state_conv, state_pool, c_prompt, c_sample,
              w_ada, b_ada, g_ffn1, w_ffn1_in, w_ffn1_out, g_mix, w_in, w_dw, b_dw, ln_conv_g, ln_conv_b,
              w_conv_out, w_pool_grp, pool_scale, w_pool_out, g_q, g_k, lam_q1, lam_k1, lam_q2, lam_k2,
              g_sub, w_attn_out, w_out, g_ffn2, w_ffn2_in, w_ffn2_out):
    Bp, Tp = x_prompt.shape[0], x_prompt.shape[1]
    past = cache_attn_k.shape[2]
    pos_p = jnp.arange(Tp)
    pos_s = past + jnp.arange(x_sample.shape[1])
    conv0 = jnp.zeros((Bp, CONV_K - 1, CONV_W), x_prompt.dtype)
    pool0 = jnp.zeros((Bp, POOL_BUF, POOL_W), x_prompt.dtype)
    xp, xs = x_prompt, x_sample
    kp_l, vp_l, cp_l, pp_l, ks_l, vs_l, cs_l, ps_l = [], [], [], [], [], [], [], []
    for l in range(DEPTH):
        lam_init = 0.8 - 0.6 * math.exp(-0.3 * l)
        lam = (jnp.exp(jnp.sum(lam_q1[l].astype(jnp.float32) * lam_k1[l].astype(jnp.float32)))
               - jnp.exp(jnp.sum(lam_q2[l].astype(jnp.float32) * lam_k2[l].astype(jnp.float32))) + lam_init)
        p = dict(w_ada=w_ada[l], b_ada=b_ada[l], g_ffn1=g_ffn1[l], w_ffn1_in=w_ffn1_in[l], w_ffn1_out=w_ffn1_out[l],
                 g_mix=g_mix[l], w_in=w_in[l], w_dw=w_dw[l], b_dw=b_dw[l], ln_conv_g=ln_conv_g[l],
                 ln_conv_b=ln_conv_b[l], w_conv_out=w_conv_out[l], w_pool_grp=w_pool_grp[l],
                 pool_scale=pool_scale[l], w_pool_out=w_pool_out[l], g_q=g_q[l], g_k=g_k[l], g_sub=g_sub[l],
                 w_attn_out=w_attn_out[l], w_out=w_out[l], g_ffn2=g_ffn2[l], w_ffn2_in=w_ffn2_in[l],
                 w_ffn2_out=w_ffn2_out[l])
        xp, kp, vp, cp, pp = _layer(xp, c_prompt, pos_p, conv0, pool0, None, None, p, lam, lam_init)
        xs, kn, vn, cn, pn = _layer(xs, c_sample, pos_s, state_conv[l], state_pool[l],
                                    cache_attn_k[l], cache_attn_v[l], p, lam, lam_init)
        kp_l.append(kp); vp_l.append(vp); cp_l.append(cp); pp_l.append(pp)
        ks_l.append(kn); vs_l.append(vn); cs_l.append(cn); ps_l.append(pn)
    return (xp, xs,
            jnp.stack(kp_l), jnp.stack(vp_l), jnp.stack(cp_l), jnp.stack(pp_l),
            jnp.stack(ks_l), jnp.stack(vs_l), jnp.stack(cs_l), jnp.stack(ps_l))
```

```python
import contextlib
import math
import numpy as np
import concourse.bass as bass
import concourse.mybir as mybir
from concourse.bass_utils import run_bass_kernel_spmd

F32 = mybir.dt.float32
BF16 = mybir.dt.bfloat16
AF = mybir.ActivationFunctionType
ALU = mybir.AluOpType

CFG = dict(SEQ=8192, DEPTH=4, PAST=4096, BATCH=4, DEC_BATCH=16, DEC_SEQ=16)
D = 1024; DC = 8; DFF = 2816; FC = 22; NH = 8; HD = 64
CW = 512; CK = 31; PW = 512; PB = 15; EPS = 1e-6
POOL_WINDOWS = (2, 4, 8, 16)
TT = 512
KP = 1024
SLOT = 5120
NSLOT = 4
NVEC = 24 + 16 + 124 + 3 + 72
V_G1, V_GM, V_G2, V_BDW, V_LNG, V_LNB, V_PSC, V_WDW, V_GQ, V_GK, V_GS, V_BADA = 0, 8, 16, 24, 28, 32, 36, 40, 164, 165, 166, 167
SEM_ROT = 30000


def layer_blocks():
    blocks = []

    def cols(s, n):
        return np.arange(s, s + n)

    def ffn_in(name):
        for j in range(11):
            c = np.concatenate([cols(256 * j, 128), cols(256 * j + 128, 128), cols(DFF + 256 * j, 128), cols(DFF + 256 * j + 128, 128)])
            blocks.append(("ffn_in", [(name, 0, 8, c)]))

    def ffn_out(name):
        for j in range(8):
            blocks.append(("ffn_out", [(name, 0, 22, cols(128 * j, 128))]))

    ffn_in("w_ffn1_in"); ffn_out("w_ffn1_out")
    for i in range(2):
        c = np.concatenate([cols(256 * i, 128), cols(512 + 256 * i, 128), cols(256 * i + 128, 128), cols(512 + 256 * i + 128, 128)])
        blocks.append(("conv_in", [("w_in", 0, 8, c)]))
    blocks.append(("pool_in", [("w_in", 0, 8, cols(1024, 512))]))
    blocks.append(("pool_grp", [("w_pool_grp", g * 128, 1, cols(0, 128)) for g in range(4)]))
    for i in range(2):
        blocks.append(("q_in", [("w_in", 0, 8, cols(1536 + 512 * i, 512))]))
    for i in range(2):
        blocks.append(("k_in", [("w_in", 0, 8, cols(2560 + 512 * i, 512))]))
    for i in range(2):
        blocks.append(("v_in", [("w_in", 0, 8, cols(3584 + 512 * i, 512))]))
    for c in range(8):
        blocks.append(("merge", [("w_in", 0, 8, cols(4608 + 128 * c, 128)), ("w_in", 0, 8, cols(5632 + 128 * c, 128)),
                                 ("w_in", 0, 8, cols(6656 + 128 * c, 128)), ("w_conv_out", 0, 4, cols(128 * c, 128)),
                                 ("w_pool_out", 0, 4, cols(128 * c, 128)), ("w_attn_out", 0, 8, cols(128 * c, 128))]))
    for i in range(2):
        blocks.append(("w_out", [("w_out", 0, 8, cols(512 * i, 512))]))
    ffn_in("w_ffn2_in"); ffn_out("w_ffn2_out")
    return blocks


BLOCKS = layer_blocks()
BLK_SZ = [sum(kc * len(c) for (_, _, kc, c) in parts) for (_, parts) in BLOCKS]
BLK_OFF = [0] + list(np.cumsum(BLK_SZ))
WPL = int(BLK_OFF[-1])
NPIECE = 10


def piece_of_block():
    nb = len(BLOCKS)
    per = (nb + NPIECE - 1) // NPIECE
    return [min(b // per, NPIECE - 1) for b in range(nb)], per


class Tok:
    __slots__ = ("si", "val")

    def __init__(self, si, val):
        self.si = si; self.val = val


class Buf:
    __slots__ = ("w", "r")

    def __init__(self):
        self.w = None; self.r = {}


class Builder:
    def __init__(self, nc, es):
        self.nc = nc; self.es = es
        self.E = {"pe": nc.tensor, "act": nc.scalar, "dve": nc.vector, "pool": nc.gpsimd, "sp": nc.sync}
        self.sems = []; self.owner = []; self.final = []
        self.est = {}
        self.waited = {e: {} for e in self.E}
        for e in ("pe", "act", "dve", "pool"):
            self.est[e] = [self.new_sem(e), 0]
        self.dsems = {}
        for q, n in (("sp", 8), ("pool", 8)):
            self.dsems[q] = [[self.new_sem("dma"), 0] for _ in range(n)]
        self.drr = {"sp": 0, "pool": 0}
        self.nins = 0

    def new_sem(self, owner):
        s = self.es.enter_context(self.nc.semaphore(f"s{len(self.sems)}"))
        self.sems.append(s); self.owner.append(owner); self.final.append(0)
        return len(self.sems) - 1

    def need(self, e, tok, raw=True):
        if tok is None:
            return
        self.need_sv(e, tok.si, tok.val, raw)

    def need_sv(self, e, si, val, raw):
        ow = self.owner[si]
        if ow == e and e == "pe":
            return
        d = self.waited[e]
        if d.get(si, 0) >= val:
            return
        self.E[e].wait_ge(self.sems[si], val)
        d[si] = val
        self.nins += 1

    def mark(self, e, ins):
        st = self.est[e]
        if st[1] >= SEM_ROT:
            st[0] = self.new_sem(e); st[1] = 0
        st[1] += 1
        ins.then_inc(self.sems[st[0]], 1)
        self.final[st[0]] = st[1]
        self.nins += 1
        return Tok(st[0], st[1])

    def hazards(self, e, reads, writes):
        for b in reads:
            self.need(e, b.w, True)
        for b in writes:
            self.need(e, b.w, False)
            for si, val in b.r.items():
                self.need_sv(e, si, val, False)

    def commit(self, tok, reads, writes):
        for b in reads:
            if b.r.get(tok.si, 0) < tok.val:
                b.r[tok.si] = tok.val
        for b in writes:
            b.w = tok; b.r = {}

    def op(self, e, fn, reads=(), writes=()):
        self.hazards(e, reads, writes)
        tok = self.mark(e, fn())
        self.commit(tok, reads, writes)
        return tok

    def pe_group(self, items, reads, writes):
        self.hazards("pe", reads, writes)
        n = len(items)
        ins = None
        for i, (o, l, r) in enumerate(items):
            ins = self.nc.tensor.matmul(o, l, r, start=(i == 0), stop=(i == n - 1))
        self.nins += n - 1
        tok = self.mark("pe", ins)
        self.commit(tok, reads, writes)
        return tok

    def dma(self, q, out, in_, reads=(), writes=()):
        pool = self.dsems[q]
        i = self.drr[q]; self.drr[q] = (i + 1) % len(pool)
        si, cnt = pool[i]
        if cnt > 0:
            self.need_sv(q, si, cnt, True)
        self.hazards(q, reads, writes)
        self.E[q].dma_start(out=out, in_=in_).then_inc(self.sems[si], 16)
        pool[i][1] = cnt + 16
        self.final[si] = cnt + 16
        tok = Tok(si, cnt + 16)
        self.commit(tok, reads, writes)
        self.nins += 1
        return tok

    def barrier(self, e):
        for si, v in enumerate(self.final):
            if v > 0:
                self.need_sv(e, si, v, True)


class STile:
    def __init__(self, B, name, shape, dtype, nbuf=None):
        self.t = B.es.enter_context(B.nc.sbuf_tensor("sb_" + name, list(shape), dtype))
        n = nbuf if nbuf is not None else (shape[1] if len(shape) == 3 else 1)
        self.b = [Buf() for _ in range(n)]


def build_program(cfg):
    SEQ, L, PAST, DS = cfg["SEQ"], cfg["DEPTH"], cfg["PAST"], cfg["DEC_SEQ"]
    NT = SEQ // TT
    TS = 2 * DS
    nc = bass.Bass("TRN2", target_bir_lowering=False)
    dt = nc.dram_tensor

    def din(name, shape, dtype=F32):
        return dt(name, list(shape), dtype, kind="ExternalInput").ap()

    def dout(name, shape, dtype=F32):
        return dt(name, list(shape), dtype, kind="ExternalOutput").ap()

    xT = din("xT", [D, SEQ]); xsT = din("xsT", [D, TS])
    ckT = din("ckT", [L, 2, NH, 128, PAST]); cv = din("cv", [L, 2, PAST, D])
    sconv = din("sconv", [L, 2, CW, CK - 1]); spool = din("spool", [L, 2, PW, PB])
    cT_d = din("cT", [128, DC, 3]); vecs_d = din("vecs", [128, L, NVEC]); lam_d = din("lamv", [128, L, 4, HD])
    rotm_d = din("rotm", [128, 128]); ropep_d = din("ropep", [128, 2, SEQ]); ropes_d = din("ropes", [128, 2, TS])
    wada_d = din("wada", [L, 18, 128, 4096]); wflat_d = din("wflat", [L, 128, WPL])
    yT = dout("yT", [D, SEQ]); ysT = dout("ysT", [D, TS])
    kTo = dout("kTo", [L, D, SEQ]); vo = dout("vo", [L, SEQ, D]); convo = dout("convo", [L, CW, CK - 1]); poolo = dout("poolo", [L, PW, PB])
    ksTo = dout("ksTo", [L, D, TS]); vso = dout("vso", [L, TS, D]); convso = dout("convso", [L, 2, CW, CK - 1]); poolso = dout("poolso", [L, 2, PW, PB])
    wsc = dt("wsc", [L, 128, WPL], BF16, kind="Internal").ap()
    kscr = dt("kscr", [L, NH, 128, SEQ], BF16, kind="Internal").ap()
    vscr = dt("vscr", [L, SEQ, D], BF16, kind="Internal").ap()
    kscr_s = dt("kscrs", [L, NH, 128, TS], BF16, kind="Internal").ap()
    vscr_s = dt("vscrs", [L, TS, D], BF16, kind="Internal").ap()

    es = contextlib.ExitStack()
    with es:
        B = Builder(nc, es)
        E = B.E
        x = STile(B, "x", [128, DC, TT], F32)
        h = STile(B, "h", [128, DC, TT], BF16)
        g = STile(B, "g", [128, FC, TT], BF16)
        wring = [STile(B, f"wr{i}", [128, SLOT], BF16) for i in range(NSLOT)]
        NF = 6
        fs = [STile(B, f"fs{i}", [128, 544], F32) for i in range(NF)]
        NBS = 4
        bs = [STile(B, f"bs{i}", [128, TT], BF16) for i in range(NBS)]
        zc = STile(B, "zc", [128, 4, TT + 32], F32)
        cacc = STile(B, "cacc", [128, 4, TT], F32)
        pc = STile(B, "pc", [128, 4, TT + 16], F32)
        dpl = STile(B, "dpl", [128, 4, TT], BF16)
        mpool = STile(B, "mpool", [128, 4, TT], BF16)
        vst = STile(B, "vst", [128, 4, 512], F32, nbuf=1)
        vbf = STile(B, "vbf", [128, 4, 512], BF16, nbuf=1)
        rope = STile(B, "rope", [128, 2, TT], F32, nbuf=1)
        kTp = [STile(B, f"kTp{i}", [128, KP], BF16) for i in range(2)]
        vp = [STile(B, f"vp{i}", [128, KP // 128, 128], BF16, nbuf=1) for i in range(2)]
        onesD = STile(B, "onesD", [128, 128], F32); ones512 = STile(B, "ones512", [128, 128], F32)
        ones128 = STile(B, "ones128", [128, 128], F32); blk64 = STile(B, "blk64", [128, 128], F32)
        rotm = STile(B, "rotm", [128, 128], F32); onesb = STile(B, "onesb", [128, 128], BF16)
        vecs = STile(B, "vecs", [128, L, NVEC], F32, nbuf=1)
        coef = STile(B, "coef", [128, L * 9 * DC * 3], F32)
        modT = STile(B, "modT", [128, L, 72 * 3], F32, nbuf=1)
        lamt = STile(B, "lamt", [128, L, 4, HD], F32, nbuf=1)
        lams = STile(B, "lams", [128, 16 * L], F32)
        gsub = STile(B, "gsubs", [128, L], F32)
        cTt = STile(B, "cTt", [128, DC, 3], F32, nbuf=1)
        siluc = STile(B, "siluc", [128, DC, 4], BF16, nbuf=1)
        haloc = STile(B, "haloc", [128, L, 4 * (CK - 1)], F32)
        halop = STile(B, "halop", [128, L, 4 * PB], F32)
        invc = STile(B, "invc", [128, 4, 16], F32, nbuf=1)
        banks = [B.es.enter_context(nc.psum_tensor(f"pb{i}", [128, 512], F32)) for i in range(8)]
        bbuf = [Buf() for _ in range(8)]
        st = {"ring": [0, 0], "f": 0, "b": 0, "w": 0, "rs": 0}

        def bank(r=0):
            i = st["ring"][r]; st["ring"][r] = (i + 1) % 4
            k = r * 4 + i
            return banks[k], bbuf[k]

        def fscr():
            i = st["f"]; st["f"] = (i + 1) % NF
            return fs[i].t, fs[i].b[0]

        def bscr():
            i = st["b"]; st["b"] = (i + 1) % NBS
            return bs[i].t, bs[i].b[0]

        def act(out, in_, func, reads, writes, bias=0.0, scale=1.0):
            return B.op("act", lambda: nc.scalar.activation(out=out, in_=in_, func=func, bias=bias, scale=scale), reads, writes)

        def tt(e, out, in0, in1, op, reads, writes):
            return B.op(e, lambda: E[e].tensor_tensor(out=out, in0=in0, in1=in1, op=op), reads, writes)

        def ts(e, out, in0, s1, s2, op0, op1, reads, writes):
            if s2 is None:
                return B.op(e, lambda: E[e].tensor_scalar(out=out, in0=in0, scalar1=s1, scalar2=None, op0=op0), reads, writes)
            return B.op(e, lambda: E[e].tensor_scalar(out=out, in0=in0, scalar1=s1, scalar2=s2, op0=op0, op1=op1), reads, writes)

        def stt(out, in0, s, in1, op0, op1, reads, writes):
            return B.op("dve", lambda: nc.vector.scalar_tensor_tensor(out=out, in0=in0, scalar=s, in1=in1, op0=op0, op1=op1), reads, writes)

        def mset(e, ap, val, writes):
            return B.op(e, lambda: E[e].memset(ap, val), (), writes)

        rsr = [STile(B, f"rsr{i}", [128, TT], F32) for i in range(2)]

        def rstd_from(bk, bkb, n, eps=EPS):
            i = st["rs"]; st["rs"] = (i + 1) % 2
            ft, fb = rsr[i].t, rsr[i].b[0]
            act(ft[:, 0:n], bk[:, 0:n], AF.Ln, [bkb], [fb], bias=epsb.t[:, 0:1])
            act(ft[:, 0:n], ft[:, 0:n], AF.Exp, [fb], [fb], scale=-0.5)
            return ft, fb

        epsb = STile(B, "epsb", [128, 1], F32)

        mset("dve", epsb.t[:], EPS, epsb.b)
        mset("dve", onesD.t[:], 1.0 / D, onesD.b)
        mset("dve", ones512.t[:], 1.0 / CW, ones512.b)
        mset("dve", ones128.t[:], 1.0 / 128, ones128.b)
        mset("dve", blk64.t[:], 0.0, blk64.b)
        mset("dve", blk64.t[0:64, 0:64], 1.0 / HD, blk64.b)
        mset("dve", blk64.t[64:128, 64:128], 1.0 / HD, blk64.b)
        mset("pool", onesb.t[:], 1.0, onesb.b)
        mset("pool", haloc.t[:], 0.0, haloc.b)
        mset("pool", halop.t[:], 0.0, halop.b)
        for gi, w in enumerate(POOL_WINDOWS):
            mset("pool", invc.t[:, gi, :], 1.0 / w, invc.b)
            for t_ in range(w - 1):
                mset("pool", invc.t[:, gi, t_:t_ + 1], 1.0 / (t_ + 1), invc.b)
        B.dma("sp", rotm.t[:], rotm_d[:, :], (), rotm.b)
        B.dma("sp", vecs.t[:], vecs_d[:, :, :], (), vecs.b)
        B.dma("sp", lamt.t[:], lam_d[:, :, :, :], (), lamt.b)
        B.dma("sp", cTt.t[:], cT_d[:, :, :], (), cTt.b)

        for l in range(L):
            lam_init = 0.8 - 0.6 * math.exp(-0.3 * l)
            ft, fb = fscr()
            for j in range(2):
                tt("dve", ft[:, j * 64:(j + 1) * 64], lamt.t[:, l, 2 * j, :], lamt.t[:, l, 2 * j + 1, :], ALU.mult, lamt.b, [fb])
                B.op("dve", lambda j=j: nc.vector.reduce_sum(out=lams.t[:, 4 * l + 1 + j:4 * l + 2 + j], in_=ft[:, j * 64:(j + 1) * 64], axis=mybir.AxisListType.X), [fb], lams.b)
                act(lams.t[:, 4 * l + 1 + j:4 * l + 2 + j], lams.t[:, 4 * l + 1 + j:4 * l + 2 + j], AF.Exp, lams.b, lams.b)
            tt("dve", lams.t[:, 4 * l + 3:4 * l + 4], lams.t[:, 4 * l + 2:4 * l + 3], lams.t[:, 4 * l + 1:4 * l + 2], ALU.subtract, lams.b, lams.b)
            ts("dve", lams.t[:, 4 * l:4 * l + 1], lams.t[:, 4 * l + 3:4 * l + 4], -lam_init, None, ALU.add, None, lams.b, lams.b)
            ts("dve", gsub.t[:, l:l + 1], vecs.t[:, l, V_GS:V_GS + 1], 1.0 - lam_init, None, ALU.mult, None, vecs.b, gsub.b)

        if cfg.get("DEBUG") == "consts":
            B.barrier("sp")
            return nc
        act(siluc.t[:, :, 0:3], cTt.t[:, :, :], AF.Silu, cTt.b, siluc.b)
        for l in range(L):
            for bi in range(18):
                sl = wring[st["w"] % NSLOT]; st["w"] += 1
                B.dma("pool", sl.t[:, 0:4096], wada_d[l, bi, :, :], (), sl.b)
                bk, bkb = bank(0)
                for j in range(4):
                    items = [(bk[:, 3 * j:3 * j + 3], sl.t[:, kc * 512 + j * 128:kc * 512 + (j + 1) * 128], siluc.t[:, kc, 0:3]) for kc in range(DC)]
                    B.pe_group(items, [sl.b[0], siluc.b[0]], [bkb])
                B.op("dve", lambda bk=bk, bi=bi, l=l: nc.vector.tensor_copy(out=modT.t[:, l, bi * 12:(bi + 1) * 12], in_=bk[:, 0:12]), [bkb], modT.b)
            mv = modT.t[:, l, :].rearrange("p (c s) -> p c s", s=3)
            for s in range(3):
                tt("dve", mv[:, :, s], mv[:, :, s], vecs.t[:, l, V_BADA:V_BADA + 72], ALU.add, modT.b + vecs.b, modT.b)
        cf = coef.t[:, :].rearrange("p (l k c s) -> p l k c s", l=L, k=9, c=DC)
        for l in range(L):
            mv = modT.t[:, l, :].rearrange("p (k c s) -> p k c s", k=9, c=DC)
            for i, (gv, gmul) in enumerate(((V_G1, 0.5), (V_GM, 1.0), (V_G2, 0.5))):
                for s in range(3):
                    stt(cf[:, l, 3 * i, :, s], mv[:, 3 * i + 1, :, s], 1.0, vecs.t[:, l, gv:gv + DC], ALU.add, ALU.mult, modT.b + vecs.b, coef.b)
                    B.op("dve", lambda l=l, i=i, s=s, mv=mv: nc.vector.tensor_copy(out=cf[:, l, 3 * i + 1, :, s], in_=mv[:, 3 * i, :, s]), modT.b, coef.b)
                    ts("dve", cf[:, l, 3 * i + 2, :, s], mv[:, 3 * i + 2, :, s], gmul, None, ALU.mult, None, modT.b, coef.b)

        if cfg.get("DEBUG") == "mod":
            B.barrier("sp")
            return nc
        pob, per = piece_of_block()
        wpiece = [[Buf() for _ in range(NPIECE)] for _ in range(L)]
        for l in range(L):
            for p in range(NPIECE):
                b0, b1 = p * per, min((p + 1) * per, len(BLOCKS))
                if b0 >= b1:
                    continue
                a, b_ = int(BLK_OFF[b0]), int(BLK_OFF[b1])
                B.dma("pool", wsc[l, :, a:b_], wflat_d[l, :, a:b_], (), [wpiece[l][p]])

        if cfg.get("DEBUG") == "wcast":
            B.barrier("sp")
            return nc
        wst = {"l": 0, "i": 0}

        def next_block(l, label):
            i = wst["i"]
            assert BLOCKS[i][0] == label, (BLOCKS[i][0], label)
            wst["i"] = (i + 1) % len(BLOCKS)
            sl = wring[st["w"] % NSLOT]; st["w"] += 1
            off, sz = int(BLK_OFF[i]), int(BLK_SZ[i])
            B.dma("sp", sl.t[:, 0:sz], wsc[l, :, off:off + sz], [wpiece[l][pob[i]]], sl.b)
            parts = []
            po = 0
            for (_, _, kc, c) in BLOCKS[i][1]:
                parts.append((po, kc, len(c)))
                po += kc * len(c)
            return sl, parts

        def wap(sl, part, kc, c0, n):
            po, KC, ncol = part
            a = po + kc * ncol + c0
            return sl.t[:, a:a + n]

        kscr_b = [Buf() for _ in range(L)]; vscr_b = [Buf() for _ in range(L)]
        kscrs_b = [Buf() for _ in range(L)]; vscrs_b = [Buf() for _ in range(L)]
        outb = Buf()

        class TileCtx:
            pass

        def out_dma(out, in_, reads):
            ob = Buf()
            return B.dma("pool", out, in_, reads, [ob])

        def rms_mod(tc, l, kidx):
            n = tc.T
            bk, bkb = bank(0)
            for c in range(DC):
                ft, fb = fscr()
                act(ft[:, 0:n], x.t[:, c, 0:n], AF.Square, [x.b[c]], [fb])
                B.op("pe", lambda c=c, ft=ft: nc.tensor.matmul(bk[:, 0:n], onesD.t[:, :], ft[:, 0:n], start=(c == 0), stop=(c == DC - 1)), [fb, onesD.b[0]], [bkb])
            rt, rb = rstd_from(bk, bkb, n)
            for c in range(DC):
                ft, fb = fscr()
                tt("pool" if c % 2 else "dve", ft[:, 0:n], x.t[:, c, 0:n], rt[:, 0:n], ALU.mult, [x.b[c], rb], [fb])
                for (c0, ln, sg, _) in tc.segs:
                    act(h.t[:, c, c0:c0 + ln], ft[:, c0:c0 + ln], AF.Identity, [fb, coef.b[0]], [h.b[c]],
                        bias=cf[:, l, 3 * kidx + 1, c, sg:sg + 1], scale=cf[:, l, 3 * kidx, c, sg:sg + 1])

        def resid_update(tc, l, kidx, c, bk, bkb):
            for (c0, ln, sg, _) in tc.segs:
                stt(x.t[:, c, c0:c0 + ln], bk[:, c0:c0 + ln], cf[:, l, 3 * kidx + 2, c, sg:sg + 1], x.t[:, c, c0:c0 + ln], ALU.mult, ALU.add,
                    [bkb, x.b[c], coef.b[0]], [x.b[c]])

        def ffn(tc, l, kidx):
            n = tc.T
            rms_mod(tc, l, kidx)
            for j in range(11):
                sl, parts = next_block(l, "ffn_in")
                for cc in range(2):
                    ba, bab = bank(0); bb, bbb = bank(1)
                    B.pe_group([(ba[:, 0:n], wap(sl, parts[0], kc, cc * 128, 128), h.t[:, kc, 0:n]) for kc in range(DC)], [sl.b[0]] + h.b, [bab])
                    B.pe_group([(bb[:, 0:n], wap(sl, parts[0], kc, 256 + cc * 128, 128), h.t[:, kc, 0:n]) for kc in range(DC)], [sl.b[0]] + h.b, [bbb])
                    ft, fb = fscr()
                    act(ft[:, 0:n], ba[:, 0:n], AF.Silu, [bab], [fb])
                    tt("dve", g.t[:, 2 * j + cc, 0:n], ft[:, 0:n], bb[:, 0:n], ALU.mult, [fb, bbb], [g.b[2 * j + cc]])
            for oc in range(8):
                sl, parts = next_block(l, "ffn_out")
                bo, bob = bank(oc % 2)
                B.pe_group([(bo[:, 0:n], wap(sl, parts[0], kc, 0, 128), g.t[:, kc, 0:n]) for kc in range(FC)], [sl.b[0]] + g.b, [bob])
                resid_update(tc, l, kidx, oc, bo, bob)

        def mixer(tc, l):
            n = tc.T
            rms_mod(tc, l, 1)
            qT_b = g.b[0:8]; on_b = g.b[8:16]; ya_b = g.b[16:20]
            for (c0, ln, sg, pos0) in tc.segs:
                zo = tc.zoff(sg)
                if tc.sample:
                    B.dma("pool", zc.t[:, :, zo:zo + CK - 1], sconv[l, sg - 1].rearrange("(c p) j -> p c j", p=128), (), zc.b)
                else:
                    B.op("pool", lambda zo=zo: nc.gpsimd.tensor_copy(out=zc.t[:, :, zo:zo + CK - 1], in_=haloc.t[:, l, :].rearrange("p (c j) -> p c j", c=4)), haloc.b, zc.b)
            for i in range(2):
                sl, parts = next_block(l, "conv_in")
                for cc in range(2):
                    c = 2 * i + cc
                    ba, bab = bank(0); bg_, bgb = bank(1)
                    B.pe_group([(ba[:, 0:n], wap(sl, parts[0], kc, (2 * cc) * 128, 128), h.t[:, kc, 0:n]) for kc in range(DC)], [sl.b[0]] + h.b, [bab])
                    B.pe_group([(bg_[:, 0:n], wap(sl, parts[0], kc, (2 * cc + 1) * 128, 128), h.t[:, kc, 0:n]) for kc in range(DC)], [sl.b[0]] + h.b, [bgb])
                    ft, fb = fscr()
                    act(ft[:, 0:n], bg_[:, 0:n], AF.Sigmoid, [bgb], [fb])
                    for (c0, ln, sg, pos0) in tc.segs:
                        zo = tc.zoff(sg) + CK - 1
                        tt("dve", zc.t[:, c, zo:zo + ln], ft[:, c0:c0 + ln], ba[:, c0:c0 + ln], ALU.mult, [fb, bab], [zc.b[c]])
            for (c0, ln, sg, pos0) in tc.segs:
                zo = tc.zoff(sg)
                if tc.sample:
                    out_dma(convso[l, sg - 1].rearrange("(c p) j -> p c j", p=128), zc.t[:, :, zo + ln:zo + ln + CK - 1], zc.b)
                else:
                    if tc.last:
                        out_dma(convo[l].rearrange("(c p) j -> p c j", p=128), zc.t[:, :, zo + ln:zo + ln + CK - 1], zc.b)
                    else:
                        B.op("pool", lambda zo=zo, ln=ln: nc.gpsimd.tensor_copy(out=haloc.t[:, l, :].rearrange("p (c j) -> p c j", c=4), in_=zc.t[:, :, zo + ln:zo + ln + CK - 1]), zc.b, haloc.b)
                for j in range(CK):
                    for c in range(4):
                        wj = vecs.t[:, l, V_WDW + c * CK + j:V_WDW + c * CK + j + 1]
                        if j == 0:
                            ts("dve", cacc.t[:, c, c0:c0 + ln], zc.t[:, c, zo:zo + ln], wj, vecs.t[:, l, V_BDW + c:V_BDW + c + 1], ALU.mult, ALU.add,
                               [zc.b[c], vecs.b[0]], [cacc.b[c]])
                        else:
                            stt(cacc.t[:, c, c0:c0 + ln], zc.t[:, c, zo + j:zo + j + ln], wj, cacc.t[:, c, c0:c0 + ln], ALU.mult, ALU.add,
                                [zc.b[c], cacc.b[c], vecs.b[0]], [cacc.b[c]])
            bm, bmb = bank(0)
            for c in range(4):
                B.op("pe", lambda c=c: nc.tensor.matmul(bm[:, 0:n], ones512.t[:, :], cacc.t[:, c, 0:n], start=(c == 0), stop=(c == 3)), [cacc.b[c], ones512.b[0]], [bmb])
            for c in range(4):
                tt("dve", cacc.t[:, c, 0:n], cacc.t[:, c, 0:n], bm[:, 0:n], ALU.subtract, [cacc.b[c], bmb], [cacc.b[c]])
            bv, bvb = bank(1)
            for c in range(4):
                ft, fb = fscr()
                act(ft[:, 0:n], cacc.t[:, c, 0:n], AF.Square, [cacc.b[c]], [fb])
                B.op("pe", lambda c=c, ft=ft: nc.tensor.matmul(bv[:, 0:n], ones512.t[:, :], ft[:, 0:n], start=(c == 0), stop=(c == 3)), [fb, ones512.b[0]], [bvb])
            rt, rb = rstd_from(bv, bvb, n)
            for c in range(4):
                ft, fb = fscr()
                tt("pool", ft[:, 0:n], cacc.t[:, c, 0:n], rt[:, 0:n], ALU.mult, [cacc.b[c], rb], [fb])
                act(g.t[:, 16 + c, 0:n], ft[:, 0:n], AF.Silu, [fb, vecs.b[0]], [g.b[16 + c]],
                    bias=vecs.t[:, l, V_LNB + c:V_LNB + c + 1], scale=vecs.t[:, l, V_LNG + c:V_LNG + c + 1])
            if cfg.get("DEBUG") == "conv":
                return
            for (c0, ln, sg, pos0) in tc.segs:
                po_ = tc.poff(sg)
                if tc.sample:
                    B.dma("pool", pc.t[:, :, po_:po_ + PB], spool[l, sg - 1].rearrange("(c p) j -> p c j", p=128), (), pc.b)
                else:
                    B.op("pool", lambda po_=po_: nc.gpsimd.tensor_copy(out=pc.t[:, :, po_:po_ + PB], in_=halop.t[:, l, :].rearrange("p (c j) -> p c j", c=4)), halop.b, pc.b)
            sl, parts = next_block(l, "pool_in")
            for c in range(4):
                bp, bpb = bank(c % 2)
                B.pe_group([(bp[:, 0:n], wap(sl, parts[0], kc, c * 128, 128), h.t[:, kc, 0:n]) for kc in range(DC)], [sl.b[0]] + h.b, [bpb])
                for (c0, ln, sg, pos0) in tc.segs:
                    po_ = tc.poff(sg) + PB
                    act(pc.t[:, c, po_:po_ + ln], bp[:, c0:c0 + ln], AF.Copy, [bpb], [pc.b[c]])
            for (c0, ln, sg, pos0) in tc.segs:
                po_ = tc.poff(sg)
                if tc.sample:
                    out_dma(poolso[l, sg - 1].rearrange("(c p) j -> p c j", p=128), pc.t[:, :, po_ + ln:po_ + ln + PB], pc.b)
                elif tc.last:
                    out_dma(poolo[l].rearrange("(c p) j -> p c j", p=128), pc.t[:, :, po_ + ln:po_ + ln + PB], pc.b)
                else:
                    B.op("pool", lambda po_=po_, ln=ln: nc.gpsimd.tensor_copy(out=halop.t[:, l, :].rearrange("p (c j) -> p c j", c=4), in_=pc.t[:, :, po_ + ln:po_ + ln + PB]), pc.b, halop.b)
                for gi, w in enumerate(POOL_WINDOWS):
                    cur, curb, lo = pc.t[:, gi, po_:po_ + PB + ln], pc.b[gi], 0
                    step = 1
                    while step < w:
                        nlo = PB - (w - 2 * step)
                        ft, fb = fscr()
                        cnt = PB + ln - nlo
                        tt("pool", ft[:, 0:cnt], cur[:, nlo - lo:nlo - lo + cnt], cur[:, nlo - lo - step:nlo - lo - step + cnt], ALU.add, [curb], [fb])
                        cur, curb, lo = ft, fb, nlo
                        step *= 2
                    pcur = pc.t[:, gi, po_ + PB:po_ + PB + ln]
                    stt(dpl.t[:, gi, c0:c0 + ln], cur[:, 0:ln], 1.0 / w, pcur, ALU.mult, ALU.subtract, [curb, pc.b[gi]], [dpl.b[gi]])
                    if pos0 == 0:
                        ft2, fb2 = fscr()
                        tt("dve", ft2[:, 0:16], cur[:, 0:16], invc.t[:, gi, :], ALU.mult, [curb, invc.b[0]], [fb2])
                        tt("dve", dpl.t[:, gi, c0:c0 + 16], ft2[:, 0:16], pcur[:, 0:16], ALU.subtract, [fb2, pc.b[gi]], [dpl.b[gi]])
            sl, parts = next_block(l, "pool_grp")
            for gi in range(4):
                bm2, bm2b = bank(gi % 2)
                B.pe_group([(bm2[:, 0:n], wap(sl, parts[gi], 0, 0, 128), dpl.t[:, gi, 0:n])], [sl.b[0], dpl.b[gi]], [bm2b])
                act(mpool.t[:, gi, 0:n], bm2[:, 0:n], AF.Copy, [bm2b, vecs.b[0]], [mpool.b[gi]], scale=vecs.t[:, l, V_PSC + gi:V_PSC + gi + 1])
            if cfg.get("DEBUG") == "pool":
                return
            for which in ("q_in", "k_in"):
                isq = which == "q_in"
                gvec = vecs.t[:, l, (V_GQ if isq else V_GK):(V_GQ if isq else V_GK) + 1]
                for i in range(2):
                    sl, parts = next_block(l, which)
                    for cc in range(4):
                        hh = 4 * i + cc
                        bq, bqb = bank(0)
                        B.pe_group([(bq[:, 0:n], wap(sl, parts[0], kc, cc * 128, 128), h.t[:, kc, 0:n]) for kc in range(DC)], [sl.b[0]] + h.b, [bqb])
                        ft, fb = fscr()
                        act(ft[:, 0:n], bq[:, 0:n], AF.Square, [bqb], [fb])
                        bs_, bsb = bank(1)
                        B.op("pe", lambda ft=ft, bs_=bs_: nc.tensor.matmul(bs_[:, 0:n], blk64.t[:, :], ft[:, 0:n], start=True, stop=True), [fb, blk64.b[0]], [bsb])
                        rt, rb = rstd_from(bs_, bsb, n)
                        qn, qnb = fscr()
                        stt(qn[:, 0:n], bq[:, 0:n], gvec, rt[:, 0:n], ALU.mult, ALU.mult, [bqb, rb, vecs.b[0]], [qnb])
                        br, brb = bank(1)
                        B.op("pe", lambda qn=qn, br=br: nc.tensor.matmul(br[:, 0:n], rotm.t[:, :], qn[:, 0:n], start=True, stop=True), [qnb, rotm.b[0]], [brb])
                        t1, t1b = fscr()
                        tt("pool", t1[:, 0:n], qn[:, 0:n], rope.t[:, 0, 0:n], ALU.mult, [qnb, rope.b[0]], [t1b])
                        t2, t2b = fscr()
                        tt("dve", t2[:, 0:n], br[:, 0:n], rope.t[:, 1, 0:n], ALU.mult, [brb, rope.b[0]], [t2b])
                        if isq:
                            tt("dve", g.t[:, hh, 0:n], t1[:, 0:n], t2[:, 0:n], ALU.add, [t1b, t2b], [g.b[hh]])
                        else:
                            tt("dve", t1[:, 0:n], t1[:, 0:n], t2[:, 0:n], ALU.add, [t1b, t2b], [t1b])
                            kb, kbb = bscr()
                            act(kb[:, 0:n], t1[:, 0:n], AF.Copy, [t1b], [kbb])
                            if tc.sample:
                                out_dma(ksTo[l, hh * 128:(hh + 1) * 128, :], t1[:, 0:n], [t1b])
                                B.dma("pool", kscr_s[l, hh, :, :], kb[:, 0:n], [kbb], [kscrs_b[l]])
                            else:
                                out_dma(kTo[l, hh * 128:(hh + 1) * 128, tc.t0:tc.t0 + n], t1[:, 0:n], [t1b])
                                B.dma("pool", kscr[l, hh, :, tc.t0:tc.t0 + n], kb[:, 0:n], [kbb], [kscr_b[l]])
            if cfg.get("DEBUG") == "qk":
                return
            ntb = (n + 127) // 128
            for half in range(2):
                sl, parts = next_block(l, "v_in")
                for tb in range(ntb):
                    nt_ = min(128, n - tb * 128)
                    bv2, bv2b = bank(tb % 2)
                    B.pe_group([(bv2[0:nt_, 0:512], h.t[:, kc, tb * 128:tb * 128 + nt_], wap(sl, parts[0], kc, 0, 512)) for kc in range(DC)], [sl.b[0]] + h.b, [bv2b])
                    act(vst.t[0:nt_, tb, :], bv2[0:nt_, 0:512], AF.Copy, [bv2b], vst.b)
                    B.op("pool", lambda nt_=nt_, tb=tb: nc.gpsimd.tensor_copy(out=vbf.t[0:nt_, tb, :], in_=vst.t[0:nt_, tb, :]), vst.b, vbf.b)
                cs = slice(half * 512, (half + 1) * 512)
                if cfg.get("VNODMA"):
                    continue
                if tc.sample:
                    out_dma(vso[l, :, cs], vst.t[0:n, 0, :], vst.b)
                    B.dma("pool", vscr_s[l, :, cs], vbf.t[0:n, 0, :], vbf.b, [vscrs_b[l]])
                else:
                    out_dma(vo[l, tc.t0:tc.t0 + n, cs].rearrange("(tb p) v -> p tb v", p=128), vst.t[:, 0:ntb, :], vst.b)
                    B.dma("pool", vscr[l, tc.t0:tc.t0 + n, cs].rearrange("(tb p) v -> p tb v", p=128), vbf.t[:, 0:ntb, :], vbf.b, [vscr_b[l]])
            if cfg.get("DEBUG") == "v":
                return
            for (c0, ln, sg, pos0) in tc.segs:
                for hh in range(NH):
                    attention_head(tc, l, hh, c0, ln, sg)
            if cfg.get("DEBUG") == "attn":
                return
            for c in range(8):
                sl, parts = next_block(l, "merge")
                sgs = []
                for b_ in range(3):
                    bg_, bgb = bank(b_ % 2)
                    B.pe_group([(bg_[:, 0:n], wap(sl, parts[b_], kc, 0, 128), h.t[:, kc, 0:n]) for kc in range(DC)], [sl.b[0]] + h.b, [bgb])
                    ft, fb = fscr()
                    act(ft[:, 0:n], bg_[:, 0:n], AF.Sigmoid, [bgb], [fb])
                    sgs.append((ft, fb))
                srcs = [(parts[3], 4, 16, ya_b), (parts[4], 4, None, mpool.b), (parts[5], 8, 8, on_b)]
                acc = None
                for b_, (part, kcn, goff, rb_) in enumerate(srcs):
                    by, byb = bank(b_ % 2)
                    if goff is None:
                        items = [(by[:, 0:n], wap(sl, part, kc, 0, 128), mpool.t[:, kc, 0:n]) for kc in range(kcn)]
                    else:
                        items = [(by[:, 0:n], wap(sl, part, kc, 0, 128), g.t[:, goff + kc, 0:n]) for kc in range(kcn)]
                    B.pe_group(items, [sl.b[0]] + list(rb_), [byb])
                    ft, fb = sgs[b_]
                    tt("dve", ft[:, 0:n], ft[:, 0:n], by[:, 0:n], ALU.mult, [fb, byb], [fb])
                tt("pool", sgs[0][0][:, 0:n], sgs[0][0][:, 0:n], sgs[1][0][:, 0:n], ALU.add, [sgs[0][1], sgs[1][1]], [sgs[0][1]])
                tt("dve", g.t[:, c, 0:n], sgs[0][0][:, 0:n], sgs[2][0][:, 0:n], ALU.add, [sgs[0][1], sgs[2][1]], [g.b[c]])
            for i in range(2):
                sl, parts = next_block(l, "w_out")
                for cc in range(4):
                    oc = 4 * i + cc
                    bo, bob = bank(oc % 2)
                    B.pe_group([(bo[:, 0:n], wap(sl, parts[0], kc, cc * 128, 128), g.t[:, kc, 0:n]) for kc in range(DC)], [sl.b[0]] + g.b[0:8], [bob])
                    resid_update(tc, l, 1, oc, bo, bob)

        ost = {"i": 0}

        def attention_head(tc, l, hh, c0, ln, sg):
            if tc.sample:
                srcs = [("cache", PAST), ("new", tc.T)]
            else:
                srcs = [("scr", tc.t0 + ln)]
            pieces = []
            for kind, nkeys in srcs:
                if kind == "new":
                    pieces.append((kind, c0, ln))
                else:
                    for k0 in range(0, nkeys, KP):
                        pieces.append((kind, k0, min(KP, nkeys - k0)))
            accs = [bank(1) for _ in range(4)]
            steps = []
            for pi, (kind, k0, nk) in enumerate(pieces):
                for kt in range((nk + 127) // 128):
                    nkk = min(128, nk - kt * 128)
                    kabs = k0 + kt * 128
                    qlo = 0; diag = False
                    if kind == "scr" and kabs >= tc.t0:
                        qlo = kabs - tc.t0; diag = True
                    for m in range(2):
                        steps.append((pi, kt, nkk, qlo, diag, m))
            loaded = {}

            def load_piece(pi):
                kind, k0, nk = pieces[pi]
                i = ost["i"]; ost["i"] += 1
                kt_, vt_ = kTp[i % 2], vp[i % 2]
                nkt = (nk + 127) // 128
                if kind == "scr":
                    B.dma("sp", kt_.t[:, 0:nk], kscr[l, hh, :, k0:k0 + nk], [kscr_b[l]], kt_.b)
                    B.dma("sp", vt_.t[:, 0:nkt, :], vscr[l, k0:k0 + nk, hh * 128:(hh + 1) * 128].rearrange("(j p) v -> p j v", p=128), [vscr_b[l]], vt_.b)
                elif kind == "cache":
                    B.dma("pool", kt_.t[:, 0:nk], ckT[l, sg - 1, hh, :, k0:k0 + nk], (), kt_.b)
                    B.dma("pool", vt_.t[:, 0:nkt, :], cv[l, sg - 1, k0:k0 + nk, hh * 128:(hh + 1) * 128].rearrange("(j p) v -> p j v", p=128), (), vt_.b)
                else:
                    B.dma("sp", kt_.t[:, 0:nk], kscr_s[l, hh, :, k0:k0 + nk], [kscrs_b[l]], kt_.b)
                    B.dma("sp", vt_.t[0:nk, 0, :], vscr_s[l, k0:k0 + nk, hh * 128:(hh + 1) * 128], [vscrs_b[l]], vt_.b)
                loaded[pi] = (kt_, vt_)

            def emit_score(si):
                pi, kt, nkk, qlo, diag, m = steps[si]
                if pi not in loaded:
                    load_piece(pi)
                kt_, vt_ = loaded[pi]
                sc, scb = bank(0)
                nq = ln - qlo
                B.pe_group([(sc[0:nkk, 0:nq], kt_.t[m * 64:(m + 1) * 64, kt * 128:kt * 128 + nkk], g.t[m * 64:(m + 1) * 64, hh, c0 + qlo:c0 + ln])],
                           [kt_.b[0], g.b[hh]], [scb])
                return sc, scb

            nsteps = len(steps)
            first = [True, True]
            lastidx = [max(i for i in range(nsteps) if steps[i][5] == m) for m in range(2)]
            nxt = emit_score(0)
            for si in range(nsteps):
                sc, scb = nxt
                if si + 1 < nsteps:
                    nxt = emit_score(si + 1)
                pi, kt, nkk, qlo, diag, m = steps[si]
                kt_, vt_ = loaded[pi]
                nq = ln - qlo
                pT, pTb = bscr()
                act(pT[0:nkk, 0:nq], sc[0:nkk, 0:nq], AF.Exp, [scb], [pTb], scale=HD ** -0.5)
                if diag:
                    mset("pool", pT[64:128, 0:64], 0.0, [pTb])
                (oa, oab), (sa, sab) = accs[2 * m], accs[2 * m + 1]
                fl = first[m]; first[m] = False
                la = si == lastidx[m]
                B.op("pe", lambda: nc.tensor.matmul(oa[:, qlo:ln], vt_.t[0:nkk, kt, :], pT[0:nkk, 0:nq], start=fl, stop=la), [vt_.b[0], pTb], [oab])
                B.op("pe", lambda: nc.tensor.matmul(sa[:, qlo:ln], onesb.t[0:nkk, :], pT[0:nkk, 0:nq], start=fl, stop=la), [onesb.b[0], pTb], [sab])
            os_ = []
            for m in range(2):
                (oa, oab), (sa, sab) = accs[2 * m], accs[2 * m + 1]
                rt, rb = fscr()
                act(rt[:, 0:ln], sa[:, 0:ln], AF.Ln, [sab], [rb])
                act(rt[:, 0:ln], rt[:, 0:ln], AF.Exp, [rb], [rb], scale=-1.0)
                tt("dve", rt[:, 0:ln], rt[:, 0:ln], oa[:, 0:ln], ALU.mult, [rb, oab], [rb])
                os_.append((rt, rb))
            o1, o1b = os_[0]; o2, o2b = os_[1]
            stt(o1[:, 0:ln], o2[:, 0:ln], lams.t[:, 4 * l:4 * l + 1], o1[:, 0:ln], ALU.mult, ALU.add, [o1b, o2b, lams.b[0]], [o1b])
            act(o2[:, 0:ln], o1[:, 0:ln], AF.Square, [o1b], [o2b])
            bn, bnb = bank(0)
            B.op("pe", lambda: nc.tensor.matmul(bn[:, 0:ln], ones128.t[:, :], o2[:, 0:ln], start=True, stop=True), [o2b, ones128.b[0]], [bnb])
            rt, rb = rstd_from(bn, bnb, ln)
            tt("dve", o1[:, 0:ln], o1[:, 0:ln], rt[:, 0:ln], ALU.mult, [o1b, rb], [o1b])
            act(g.t[:, 8 + hh, c0:c0 + ln], o1[:, 0:ln], AF.Copy, [o1b, gsub.b[0]], [g.b[8 + hh]], scale=gsub.t[:, l:l + 1])

        def run_tile(tc):
            n = tc.T
            if tc.sample:
                B.dma("pool", x.t[:, :, 0:n], xsT.rearrange("(c p) s -> p c s", p=128), (), x.b)
                B.dma("pool", rope.t[:, :, 0:n], ropes_d[:, :, :], (), rope.b)
            else:
                B.dma("pool", x.t[:, :, 0:n], xT[:, tc.t0:tc.t0 + n].rearrange("(c p) s -> p c s", p=128), (), x.b)
                B.dma("pool", rope.t[:, :, 0:n], ropep_d[:, :, tc.t0:tc.t0 + n], (), rope.b)
            for l in range(L):
                assert wst["i"] == 0
                ffn(tc, l, 0)
                if cfg.get("DEBUG") == "ffn1":
                    return
                mixer(tc, l)
                if cfg.get("DEBUG") in ("mixer", "conv", "pool", "qk", "v", "attn"):
                    return
                ffn(tc, l, 2)
            if tc.sample:
                out_dma(ysT.rearrange("(c p) s -> p c s", p=128), x.t[:, :, 0:n], x.b)
            else:
                out_dma(yT[:, tc.t0:tc.t0 + n].rearrange("(c p) s -> p c s", p=128), x.t[:, :, 0:n], x.b)

        for t in range(NT):
            tc = TileCtx()
            tc.T = TT; tc.t0 = t * TT; tc.sample = False; tc.last = (t == NT - 1)
            tc.segs = [(0, TT, 0, t * TT)]
            tc.zoff = lambda sg: 0
            tc.poff = lambda sg: 0
            run_tile(tc)
        if cfg.get("DEBUG"):
            B.barrier("sp")
            return nc
        tc = TileCtx()
        tc.T = TS; tc.t0 = 0; tc.sample = True; tc.last = True
        tc.segs = [(0, DS, 1, PAST), (DS, DS, 2, PAST)]
        tc.zoff = lambda sg: (sg - 1) * (CK - 1 + DS)
        tc.poff = lambda sg: (sg - 1) * (PB + DS)
        run_tile(tc)
        B.barrier("sp")
        print("instructions emitted ~", B.nins, "sems", len(B.sems), flush=True)
    return nc


def _pack_inputs(inp, cfg):
    SEQ, L, PAST, DS = cfg["SEQ"], cfg["DEPTH"], cfg["PAST"], cfg["DEC_SEQ"]
    TS = 2 * DS
    f = np.float32
    W = {k: np.asarray(inp[k]) for k in ("w_ffn1_in", "w_ffn1_out", "w_in", "w_conv_out", "w_pool_out", "w_attn_out", "w_out", "w_ffn2_in", "w_ffn2_out")}
    W["w_pool_grp"] = np.asarray(inp["w_pool_grp"]).reshape(L, 4 * 128, 128)
    wflat = np.empty((L, 128, WPL), f)
    for l in range(L):
        o = 0
        for (_, parts) in BLOCKS:
            for (name, row0, KC, c) in parts:
                sub = W[name][l][row0:row0 + KC * 128][:, c]
                n = KC * len(c)
                wflat[l, :, o:o + n] = sub.reshape(KC, 128, len(c)).transpose(1, 0, 2).reshape(128, n)
                o += n
    wada = np.ascontiguousarray(np.asarray(inp["w_ada"]).reshape(L, DC, 128, 18, 512).transpose(0, 3, 2, 1, 4).reshape(L, 18, 128, 4096))

    def fm(v, nch):
        return np.asarray(v).reshape(L, nch, 128).transpose(2, 0, 1)
    vecs = np.zeros((128, L, NVEC), f)
    vecs[:, :, V_G1:V_G1 + 8] = fm(inp["g_ffn1"], 8); vecs[:, :, V_GM:V_GM + 8] = fm(inp["g_mix"], 8); vecs[:, :, V_G2:V_G2 + 8] = fm(inp["g_ffn2"], 8)
    vecs[:, :, V_BDW:V_BDW + 4] = fm(inp["b_dw"], 4); vecs[:, :, V_LNG:V_LNG + 4] = fm(inp["ln_conv_g"], 4); vecs[:, :, V_LNB:V_LNB + 4] = fm(inp["ln_conv_b"], 4)
    vecs[:, :, V_PSC:V_PSC + 4] = fm(inp["pool_scale"], 4)
    vecs[:, :, V_WDW:V_WDW + 124] = np.asarray(inp["w_dw"]).reshape(L, CK, 4, 128).transpose(3, 0, 2, 1).reshape(128, L, 124)
    vecs[:, :, V_GQ] = np.tile(np.asarray(inp["g_q"]), (1, 2)).T; vecs[:, :, V_GK] = np.tile(np.asarray(inp["g_k"]), (1, 2)).T
    vecs[:, :, V_GS] = np.asarray(inp["g_sub"]).T
    vecs[:, :, V_BADA:V_BADA + 72] = fm(inp["b_ada"], 72)
    lamv = np.stack([np.asarray(inp[k]) for k in ("lam_q1", "lam_k1", "lam_q2", "lam_k2")], axis=1)
    lamv = np.ascontiguousarray(np.broadcast_to(lamv[None], (128, L, 4, HD))).astype(f)
    rotm = np.zeros((128, 128), f)
    for p in range(128):
        if p % 64 < 32:
            rotm[p + 32, p] = -1.0
        else:
            rotm[p - 32, p] = 1.0
    half = HD // 2
    inv = (np.float32(10000.0) ** (-np.arange(half, dtype=np.float32) / np.float32(half))).astype(f)

    def rope_tab(pos):
        ang = pos.astype(f)[None, :] * inv[:, None]
        cs = np.stack([np.cos(ang), np.sin(ang)], axis=1).astype(f)
        return np.ascontiguousarray(np.tile(cs, (4, 1, 1)))
    ropep = rope_tab(np.arange(SEQ))
    rs = rope_tab(PAST + np.arange(DS))
    ropes = np.ascontiguousarray(np.concatenate([rs, rs], axis=2))
    xp = np.asarray(inp["x_prompt"]); xs = np.asarray(inp["x_sample"])
    ck = np.asarray(inp["cache_attn_k"]); cvv = np.asarray(inp["cache_attn_v"])
    sc = np.asarray(inp["state_conv"]); sp = np.asarray(inp["state_pool"])
    cp = np.asarray(inp["c_prompt"]); csm = np.asarray(inp["c_sample"])
    shared = dict(vecs=vecs, lamv=lamv, rotm=rotm, ropep=ropep, ropes=ropes, wada=wada, wflat=wflat)
    maps = []
    for i in range(8):
        b = i % cfg["BATCH"]
        s0 = 2 * i
        m = dict(shared)
        m["xT"] = np.ascontiguousarray(xp[b].T)
        m["xsT"] = np.ascontiguousarray(xs[s0:s0 + 2].reshape(TS, D).T)
        m["ckT"] = np.ascontiguousarray(ck[:, s0:s0 + 2].reshape(L, 2, PAST, NH, 128).transpose(0, 1, 3, 4, 2))
        m["cv"] = np.ascontiguousarray(cvv[:, s0:s0 + 2].reshape(L, 2, PAST, D))
        m["sconv"] = np.ascontiguousarray(sc[:, s0:s0 + 2].transpose(0, 1, 3, 2))
        m["spool"] = np.ascontiguousarray(sp[:, s0:s0 + 2].transpose(0, 1, 3, 2))
        cc = np.stack([cp[b], csm[s0], csm[s0 + 1]], axis=0)
        m["cT"] = np.ascontiguousarray(cc.reshape(3, DC, 128).transpose(2, 1, 0))
        maps.append(m)
    return maps


def _unpack(res, cfg):
    SEQ, L, PAST, DS = cfg["SEQ"], cfg["DEPTH"], cfg["PAST"], cfg["DEC_SEQ"]
    NB, NDB = cfg["BATCH"], cfg["DEC_BATCH"]
    f = np.float32
    r = res
    y_p = np.stack([np.asarray(r[b]["yT"]).T for b in range(NB)]).astype(f)
    k_p = np.stack([np.asarray(r[b]["kTo"]).transpose(0, 2, 1) for b in range(NB)], axis=1).reshape(L, NB, SEQ, NH, 2, HD).astype(f)
    v_p = np.stack([np.asarray(r[b]["vo"]) for b in range(NB)], axis=1).reshape(L, NB, SEQ, NH, 128).astype(f)
    c_p = np.stack([np.asarray(r[b]["convo"]).transpose(0, 2, 1) for b in range(NB)], axis=1).astype(f)
    p_p = np.stack([np.asarray(r[b]["poolo"]).transpose(0, 2, 1) for b in range(NB)], axis=1).astype(f)
    y_s = np.concatenate([np.asarray(r[i]["ysT"]).T.reshape(2, DS, D) for i in range(8)], axis=0).astype(f)
    k_s = np.concatenate([np.asarray(r[i]["ksTo"]).transpose(0, 2, 1).reshape(L, 2, DS, NH, 2, HD) for i in range(8)], axis=1).astype(f)
    v_s = np.concatenate([np.asarray(r[i]["vso"]).reshape(L, 2, DS, NH, 128) for i in range(8)], axis=1).astype(f)
    c_s = np.concatenate([np.asarray(r[i]["convso"]).transpose(0, 1, 3, 2) for i in range(8)], axis=1).astype(f)
    p_s = np.concatenate([np.asarray(r[i]["poolso"]).transpose(0, 1, 3, 2) for i in range(8)], axis=1).astype(f)
    return (np.ascontiguousarray(y_p), np.ascontiguousarray(y_s), np.ascontiguousarray(k_p), np.ascontiguousarray(v_p),
            np.ascontiguousarray(c_p), np.ascontiguousarray(p_p), np.ascontiguousarray(k_s), np.ascontiguousarray(v_s),
            np.ascontiguousarray(c_s), np.ascontiguousarray(p_s))


def kernel(**inputs):
    cfg = dict(CFG)
    nc = build_program(cfg)
    maps = _pack_inputs(inputs, cfg)
    res = run_bass_kernel_spmd(nc, maps, core_ids=list(range(8)))
    return _unpack(res.results, cfg)
```

```python
import contextlib
import math
import numpy as np
import concourse.bass as bass
import concourse.mybir as mybir
from concourse.bass_utils import run_bass_kernel_spmd

F32 = mybir.dt.float32
BF16 = mybir.dt.bfloat16
AF = mybir.ActivationFunctionType
ALU = mybir.AluOpType

CFG = dict(SEQ=8192, DEPTH=4, PAST=4096, BATCH=4, DEC_BATCH=16, DEC_SEQ=16)
D = 1024; DC = 8; DFF = 2816; FC = 22; NH = 8; HD = 64
CW = 512; CK = 31; PW = 512; PB = 15; EPS = 1e-6
POOL_WINDOWS = (2, 4, 8, 16)
TT = 512
KP = 1024
SLOT = 5120
NSLOT = 4
NVEC = 24 + 16 + 124 + 3 + 72
V_G1, V_GM, V_G2, V_BDW, V_LNG, V_LNB, V_PSC, V_WDW, V_GQ, V_GK, V_GS, V_BADA = 0, 8, 16, 24, 28, 32, 36, 40, 164, 165, 166, 167
SEM_ROT = 30000


def layer_blocks():
    blocks = []

    def cols(s, n):
        return np.arange(s, s + n)

    def ffn_in(name):
        for j in range(11):
            c = np.concatenate([cols(256 * j, 128), cols(256 * j + 128, 128), cols(DFF + 256 * j, 128), cols(DFF + 256 * j + 128, 128)])
            blocks.append(("ffn_in", [(name, 0, 8, c)]))

    def ffn_out(name):
        for j in range(8):
            blocks.append(("ffn_out", [(name, 0, 22, cols(128 * j, 128))]))

    ffn_in("w_ffn1_in"); ffn_out("w_ffn1_out")
    for i in range(2):
        c = np.concatenate([cols(256 * i, 128), cols(512 + 256 * i, 128), cols(256 * i + 128, 128), cols(512 + 256 * i + 128, 128)])
        blocks.append(("conv_in", [("w_in", 0, 8, c)]))
    blocks.append(("pool_in", [("w_in", 0, 8, cols(1024, 512))]))
    blocks.append(("pool_grp", [("w_pool_grp", g * 128, 1, cols(0, 128)) for g in range(4)]))
    for i in range(2):
        blocks.append(("q_in", [("w_in", 0, 8, cols(1536 + 512 * i, 512))]))
    for i in range(2):
        blocks.append(("k_in", [("w_in", 0, 8, cols(2560 + 512 * i, 512))]))
    for i in range(2):
        blocks.append(("v_in", [("w_in", 0, 8, cols(3584 + 512 * i, 512))]))
    for c in range(8):
        blocks.append(("merge", [("w_in", 0, 8, cols(4608 + 128 * c, 128)), ("w_in", 0, 8, cols(5632 + 128 * c, 128)),
                                 ("w_in", 0, 8, cols(6656 + 128 * c, 128)), ("w_conv_out", 0, 4, cols(128 * c, 128)),
                                 ("w_pool_out", 0, 4, cols(128 * c, 128)), ("w_attn_out", 0, 8, cols(128 * c, 128))]))
    for i in range(2):
        blocks.append(("w_out", [("w_out", 0, 8, cols(512 * i, 512))]))
    ffn_in("w_ffn2_in"); ffn_out("w_ffn2_out")
    return blocks


BLOCKS = layer_blocks()
BLK_SZ = [sum(kc * len(c) for (_, _, kc, c) in parts) for (_, parts) in BLOCKS]
BLK_OFF = [0] + list(np.cumsum(BLK_SZ))
WPL = int(BLK_OFF[-1])
NPIECE = 10


def piece_of_block():
    nb = len(BLOCKS)
    per = (nb + NPIECE - 1) // NPIECE
    return [min(b // per, NPIECE - 1) for b in range(nb)], per


class Tok:
    __slots__ = ("si", "val")

    def __init__(self, si, val):
        self.si = si; self.val = val


class Buf:
    __slots__ = ("w", "r")

    def __init__(self):
        self.w = None; self.r = {}


class Builder:
    def __init__(self, nc, es):
        self.nc = nc; self.es = es
        self.E = {"pe": nc.tensor, "act": nc.scalar, "dve": nc.vector, "pool": nc.gpsimd, "sp": nc.sync}
        self.sems = []; self.owner = []; self.final = []
        self.est = {}
        self.waited = {e: {} for e in self.E}
        for e in ("pe", "act", "dve", "pool"):
            self.est[e] = [self.new_sem(e), 0]
        self.dsems = {}
        for q, n in (("sp", 8), ("pool", 8)):
            self.dsems[q] = [[self.new_sem("dma"), 0] for _ in range(n)]
        self.drr = {"sp": 0, "pool": 0}
        self.nins = 0

    def new_sem(self, owner):
        s = self.es.enter_context(self.nc.semaphore(f"s{len(self.sems)}"))
        self.sems.append(s); self.owner.append(owner); self.final.append(0)
        return len(self.sems) - 1

    def need(self, e, tok, raw=True):
        if tok is None:
            return
        self.need_sv(e, tok.si, tok.val, raw)

    def need_sv(self, e, si, val, raw):
        ow = self.owner[si]
        if ow == e and e == "pe":
            return
        d = self.waited[e]
        if d.get(si, 0) >= val:
            return
        self.E[e].wait_ge(self.sems[si], val)
        d[si] = val
        self.nins += 1

    def mark(self, e, ins):
        st = self.est[e]
        if st[1] >= SEM_ROT:
            st[0] = self.new_sem(e); st[1] = 0
        st[1] += 1
        ins.then_inc(self.sems[st[0]], 1)
        self.final[st[0]] = st[1]
        self.nins += 1
        return Tok(st[0], st[1])

    def hazards(self, e, reads, writes):
        for b in reads:
            self.need(e, b.w, True)
        for b in writes:
            self.need(e, b.w, False)
            for si, val in b.r.items():
                self.need_sv(e, si, val, False)

    def commit(self, tok, reads, writes):
        for b in reads:
            if b.r.get(tok.si, 0) < tok.val:
                b.r[tok.si] = tok.val
        for b in writes:
            b.w = tok; b.r = {}

    def op(self, e, fn, reads=(), writes=()):
        self.hazards(e, reads, writes)
        tok = self.mark(e, fn())
        self.commit(tok, reads, writes)
        return tok

    def pe_group(self, items, reads, writes):
        self.hazards("pe", reads, writes)
        n = len(items)
        ins = None
        for i, (o, l, r) in enumerate(items):
            ins = self.nc.tensor.matmul(o, l, r, start=(i == 0), stop=(i == n - 1))
        self.nins += n - 1
        tok = self.mark("pe", ins)
        self.commit(tok, reads, writes)
        return tok

    def dma(self, q, out, in_, reads=(), writes=()):
        pool = self.dsems[q]
        i = self.drr[q]; self.drr[q] = (i + 1) % len(pool)
        si, cnt = pool[i]
        if cnt > 0:
            self.need_sv(q, si, cnt, True)
        self.hazards(q, reads, writes)
        self.E[q].dma_start(out=out, in_=in_).then_inc(self.sems[si], 16)
        pool[i][1] = cnt + 16
        self.final[si] = cnt + 16
        tok = Tok(si, cnt + 16)
        self.commit(tok, reads, writes)
        self.nins += 1
        return tok

    def barrier(self, e):
        for si, v in enumerate(self.final):
            if v > 0:
                self.need_sv(e, si, v, True)


class STile:
    def __init__(self, B, name, shape, dtype, nbuf=None):
        self.t = B.es.enter_context(B.nc.sbuf_tensor("sb_" + name, list(shape), dtype))
        n = nbuf if nbuf is not None else (shape[1] if len(shape) == 3 else 1)
        self.b = [Buf() for _ in range(n)]


def build_program(cfg):
    SEQ, L, PAST, DS = cfg["SEQ"], cfg["DEPTH"], cfg["PAST"], cfg["DEC_SEQ"]
    NT = SEQ // TT
    TS = 2 * DS
    nc = bass.Bass("TRN2", target_bir_lowering=False)
    dt = nc.dram_tensor

    def din(name, shape, dtype=F32):
        return dt(name, list(shape), dtype, kind="ExternalInput").ap()

    def dout(name, shape, dtype=F32):
        return dt(name, list(shape), dtype, kind="ExternalOutput").ap()

    xT = din("xT", [D, SEQ]); xsT = din("xsT", [D, TS])
    ckT = din("ckT", [L, 2, NH, 128, PAST]); cv = din("cv", [L, 2, PAST, D])
    sconv = din("sconv", [L, 2, CW, CK - 1]); spool = din("spool", [L, 2, PW, PB])
    cT_d = din("cT", [128, DC, 3]); vecs_d = din("vecs", [128, L, NVEC]); lam_d = din("lamv", [128, L, 4, HD])
    rotm_d = din("rotm", [128, 128]); ropep_d = din("ropep", [128, 2, SEQ]); ropes_d = din("ropes", [128, 2, TS])
    wada_d = din("wada", [L, 18, 128, 4096]); wflat_d = din("wflat", [L, 128, WPL])
    yT = dout("yT", [D, SEQ]); ysT = dout("ysT", [D, TS])
    kTo = dout("kTo", [L, D, SEQ]); vo = dout("vo", [L, SEQ, D]); convo = dout("convo", [L, CW, CK - 1]); poolo = dout("poolo", [L, PW, PB])
    ksTo = dout("ksTo", [L, D, TS]); vso = dout("vso", [L, TS, D]); convso = dout("convso", [L, 2, CW, CK - 1]); poolso = dout("poolso", [L, 2, PW, PB])
    wsc = dt("wsc", [L, 128, WPL], BF16, kind="Internal").ap()
    kscr = dt("kscr", [L, NH, 128, SEQ], BF16, kind="Internal").ap()
    vscr = dt("vscr", [L, SEQ, D], BF16, kind="Internal").ap()
    kscr_s = dt("kscrs", [L, NH, 128, TS], BF16, kind="Internal").ap()
    vscr_s = dt("vscrs", [L, TS, D], BF16, kind="Internal").ap()

    es = contextlib.ExitStack()
    with es:
        B = Builder(nc, es)
        E = B.E
        x = STile(B, "x", [128, DC, TT], F32)
        h = STile(B, "h", [128, DC, TT], BF16)
        g = STile(B, "g", [128, FC, TT], BF16)
        wring = [STile(B, f"wr{i}", [128, SLOT], BF16) for i in range(NSLOT)]
        NF = 6
        fs = [STile(B, f"fs{i}", [128, 544], F32) for i in range(NF)]
        NBS = 6
        bs = [STile(B, f"bs{i}", [128, TT], BF16) for i in range(NBS)]
        zc = STile(B, "zc", [128, 4, TT + 32], F32)
        cacc = STile(B, "cacc", [128, 4, TT], F32)
        pc = STile(B, "pc", [128, 4, TT + 16], F32)
        dpl = STile(B, "dpl", [128, 4, TT], BF16)
        mpool = STile(B, "mpool", [128, 4, TT], BF16)
        vst = STile(B, "vst", [128, 4, 512], F32, nbuf=1)
        vbf = STile(B, "vbf", [128, 4, 512], BF16, nbuf=1)
        rope = STile(B, "rope", [128, 2, TT], F32, nbuf=1)
        kTp = [STile(B, f"kTp{i}", [128, KP], BF16) for i in range(2)]
        vp = [STile(B, f"vp{i}", [128, KP // 128, 128], BF16, nbuf=1) for i in range(2)]
        onesD = STile(B, "onesD", [128, 128], F32); ones512 = STile(B, "ones512", [128, 128], F32)
        ones128 = STile(B, "ones128", [128, 128], F32); blk64 = STile(B, "blk64", [128, 128], F32)
        rotm = STile(B, "rotm", [128, 128], F32); onesb = STile(B, "onesb", [128, 128], BF16)
        vecs = STile(B, "vecs", [128, L, NVEC], F32, nbuf=1)
        coef = STile(B, "coef", [128, L * 9 * DC * 3], F32)
        modT = STile(B, "modT", [128, L, 72 * 3], F32, nbuf=1)
        lamt = STile(B, "lamt", [128, L, 4, HD], F32, nbuf=1)
        lams = STile(B, "lams", [128, 16 * L], F32)
        gsub = STile(B, "gsubs", [128, L], F32)
        cTt = STile(B, "cTt", [128, DC, 3], F32, nbuf=1)
        siluc = STile(B, "siluc", [128, DC, 4], BF16, nbuf=1)
        haloc = STile(B, "haloc", [128, L, 4 * (CK - 1)], F32)
        halop = STile(B, "halop", [128, L, 4 * PB], F32)
        invc = STile(B, "invc", [128, 4, 16], F32, nbuf=1)
        banks = [B.es.enter_context(nc.psum_tensor(f"pb{i}", [128, 512], F32)) for i in range(8)]
        bbuf = [Buf() for _ in range(8)]
        st = {"ring": [0, 0], "f": 0, "b": 0, "w": 0, "rs": 0}

        def bank(r=0):
            i = st["ring"][r]; st["ring"][r] = (i + 1) % 4
            k = r * 4 + i
            return banks[k], bbuf[k]

        def fscr():
            i = st["f"]; st["f"] = (i + 1) % NF
            return fs[i].t, fs[i].b[0]

        def bscr():
            i = st["b"]; st["b"] = (i + 1) % NBS
            return bs[i].t, bs[i].b[0]

        def act(out, in_, func, reads, writes, bias=0.0, scale=1.0):
            return B.op("act", lambda: nc.scalar.activation(out=out, in_=in_, func=func, bias=bias, scale=scale), reads, writes)

        def tt(e, out, in0, in1, op, reads, writes):
            return B.op(e, lambda: E[e].tensor_tensor(out=out, in0=in0, in1=in1, op=op), reads, writes)

        def ts(e, out, in0, s1, s2, op0, op1, reads, writes):
            if s2 is None:
                return B.op(e, lambda: E[e].tensor_scalar(out=out, in0=in0, scalar1=s1, scalar2=None, op0=op0), reads, writes)
            return B.op(e, lambda: E[e].tensor_scalar(out=out, in0=in0, scalar1=s1, scalar2=s2, op0=op0, op1=op1), reads, writes)

        def stt(out, in0, s, in1, op0, op1, reads, writes):
            return B.op("dve", lambda: nc.vector.scalar_tensor_tensor(out=out, in0=in0, scalar=s, in1=in1, op0=op0, op1=op1), reads, writes)

        def mset(e, ap, val, writes):
            return B.op(e, lambda: E[e].memset(ap, val), (), writes)

        rsr = [STile(B, f"rsr{i}", [128, TT], F32) for i in range(2)]

        def rstd_from(bk, bkb, n, eps=EPS):
            i = st["rs"]; st["rs"] = (i + 1) % 2
            ft, fb = rsr[i].t, rsr[i].b[0]
            act(ft[:, 0:n], bk[:, 0:n], AF.Ln, [bkb], [fb], bias=epsb.t[:, 0:1])
            act(ft[:, 0:n], ft[:, 0:n], AF.Exp, [fb], [fb], scale=-0.5)
            return ft, fb

        epsb = STile(B, "epsb", [128, 1], F32)

        mset("dve", epsb.t[:], EPS, epsb.b)
        mset("dve", onesD.t[:], 1.0 / D, onesD.b)
        mset("dve", ones512.t[:], 1.0 / CW, ones512.b)
        mset("dve", ones128.t[:], 1.0 / 128, ones128.b)
        mset("dve", blk64.t[:], 0.0, blk64.b)
        mset("dve", blk64.t[0:64, 0:64], 1.0 / HD, blk64.b)
        mset("dve", blk64.t[64:128, 64:128], 1.0 / HD, blk64.b)
        mset("pool", onesb.t[:], 1.0, onesb.b)
        mset("pool", haloc.t[:], 0.0, haloc.b)
        mset("pool", halop.t[:], 0.0, halop.b)
        for gi, w in enumerate(POOL_WINDOWS):
            mset("pool", invc.t[:, gi, :], 1.0 / w, invc.b)
            for t_ in range(w - 1):
                mset("pool", invc.t[:, gi, t_:t_ + 1], 1.0 / (t_ + 1), invc.b)
        B.dma("sp", rotm.t[:], rotm_d[:, :], (), rotm.b)
        B.dma("sp", vecs.t[:], vecs_d[:, :, :], (), vecs.b)
        B.dma("sp", lamt.t[:], lam_d[:, :, :, :], (), lamt.b)
        B.dma("sp", cTt.t[:], cT_d[:, :, :], (), cTt.b)

        for l in range(L):
            lam_init = 0.8 - 0.6 * math.exp(-0.3 * l)
            ft, fb = fscr()
            for j in range(2):
                tt("dve", ft[:, j * 64:(j + 1) * 64], lamt.t[:, l, 2 * j, :], lamt.t[:, l, 2 * j + 1, :], ALU.mult, lamt.b, [fb])
                B.op("dve", lambda j=j: nc.vector.reduce_sum(out=lams.t[:, 4 * l + 1 + j:4 * l + 2 + j], in_=ft[:, j * 64:(j + 1) * 64], axis=mybir.AxisListType.X), [fb], lams.b)
                act(lams.t[:, 4 * l + 1 + j:4 * l + 2 + j], lams.t[:, 4 * l + 1 + j:4 * l + 2 + j], AF.Exp, lams.b, lams.b)
            tt("dve", lams.t[:, 4 * l + 3:4 * l + 4], lams.t[:, 4 * l + 2:4 * l + 3], lams.t[:, 4 * l + 1:4 * l + 2], ALU.subtract, lams.b, lams.b)
            ts("dve", lams.t[:, 4 * l:4 * l + 1], lams.t[:, 4 * l + 3:4 * l + 4], -lam_init, None, ALU.add, None, lams.b, lams.b)
            ts("dve", gsub.t[:, l:l + 1], vecs.t[:, l, V_GS:V_GS + 1], 1.0 - lam_init, None, ALU.mult, None, vecs.b, gsub.b)

        if cfg.get("DEBUG") == "consts":
            B.barrier("sp")
            return nc
        act(siluc.t[:, :, 0:3], cTt.t[:, :, :], AF.Silu, cTt.b, siluc.b)
        for l in range(L):
            for bi in range(18):
                sl = wring[st["w"] % NSLOT]; st["w"] += 1
                B.dma("pool", sl.t[:, 0:4096], wada_d[l, bi, :, :], (), sl.b)
                bk, bkb = bank(0)
                for j in range(4):
                    items = [(bk[:, 3 * j:3 * j + 3], sl.t[:, kc * 512 + j * 128:kc * 512 + (j + 1) * 128], siluc.t[:, kc, 0:3]) for kc in range(DC)]
                    B.pe_group(items, [sl.b[0], siluc.b[0]], [bkb])
                B.op("dve", lambda bk=bk, bi=bi, l=l: nc.vector.tensor_copy(out=modT.t[:, l, bi * 12:(bi + 1) * 12], in_=bk[:, 0:12]), [bkb], modT.b)
            mv = modT.t[:, l, :].rearrange("p (c s) -> p c s", s=3)
            for s in range(3):
                tt("dve", mv[:, :, s], mv[:, :, s], vecs.t[:, l, V_BADA:V_BADA + 72], ALU.add, modT.b + vecs.b, modT.b)
        cf = coef.t[:, :].rearrange("p (l k c s) -> p l k c s", l=L, k=9, c=DC)
        for l in range(L):
            mv = modT.t[:, l, :].rearrange("p (k c s) -> p k c s", k=9, c=DC)
            for i, (gv, gmul) in enumerate(((V_G1, 0.5), (V_GM, 1.0), (V_G2, 0.5))):
                for s in range(3):
                    stt(cf[:, l, 3 * i, :, s], mv[:, 3 * i + 1, :, s], 1.0, vecs.t[:, l, gv:gv + DC], ALU.add, ALU.mult, modT.b + vecs.b, coef.b)
                    B.op("dve", lambda l=l, i=i, s=s, mv=mv: nc.vector.tensor_copy(out=cf[:, l, 3 * i + 1, :, s], in_=mv[:, 3 * i, :, s]), modT.b, coef.b)
                    ts("dve", cf[:, l, 3 * i + 2, :, s], mv[:, 3 * i + 2, :, s], gmul, None, ALU.mult, None, modT.b, coef.b)

        if cfg.get("DEBUG") == "mod":
            B.barrier("sp")
            return nc
        pob, per = piece_of_block()
        wpiece = [[Buf() for _ in range(NPIECE)] for _ in range(L)]
        for l in range(L):
            for p in range(NPIECE):
                b0, b1 = p * per, min((p + 1) * per, len(BLOCKS))
                if b0 >= b1:
                    continue
                a, b_ = int(BLK_OFF[b0]), int(BLK_OFF[b1])
                B.dma("pool", wsc[l, :, a:b_], wflat_d[l, :, a:b_], (), [wpiece[l][p]])

        if cfg.get("DEBUG") == "wcast":
            B.barrier("sp")
            return nc
        wst = {"l": 0, "i": 0}

        def next_block(l, label):
            i = wst["i"]
            assert BLOCKS[i][0] == label, (BLOCKS[i][0], label)
            wst["i"] = (i + 1) % len(BLOCKS)
            sl = wring[st["w"] % NSLOT]; st["w"] += 1
            off, sz = int(BLK_OFF[i]), int(BLK_SZ[i])
            B.dma("sp", sl.t[:, 0:sz], wsc[l, :, off:off + sz], [wpiece[l][pob[i]]], sl.b)
            parts = []
            po = 0
            for (_, _, kc, c) in BLOCKS[i][1]:
                parts.append((po, kc, len(c)))
                po += kc * len(c)
            return sl, parts

        def wap(sl, part, kc, c0, n):
            po, KC, ncol = part
            a = po + kc * ncol + c0
            return sl.t[:, a:a + n]

        kscr_b = [Buf() for _ in range(L)]; vscr_b = [Buf() for _ in range(L)]
        kscrs_b = [Buf() for _ in range(L)]; vscrs_b = [Buf() for _ in range(L)]
        outb = Buf()

        class TileCtx:
            pass

        def out_dma(out, in_, reads):
            ob = Buf()
            return B.dma("pool", out, in_, reads, [ob])

        def rms_mod(tc, l, kidx):
            n = tc.T
            bk, bkb = bank(0)
            for c in range(DC):
                ft, fb = fscr()
                act(ft[:, 0:n], x.t[:, c, 0:n], AF.Square, [x.b[c]], [fb])
                B.op("pe", lambda c=c, ft=ft: nc.tensor.matmul(bk[:, 0:n], onesD.t[:, :], ft[:, 0:n], start=(c == 0), stop=(c == DC - 1)), [fb, onesD.b[0]], [bkb])
            rt, rb = rstd_from(bk, bkb, n)
            for c in range(DC):
                ft, fb = fscr()
                tt("pool" if c % 2 else "dve", ft[:, 0:n], x.t[:, c, 0:n], rt[:, 0:n], ALU.mult, [x.b[c], rb], [fb])
                for (c0, ln, sg, _) in tc.segs:
                    act(h.t[:, c, c0:c0 + ln], ft[:, c0:c0 + ln], AF.Identity, [fb, coef.b[0]], [h.b[c]],
                        bias=cf[:, l, 3 * kidx + 1, c, sg:sg + 1], scale=cf[:, l, 3 * kidx, c, sg:sg + 1])

        def resid_update(tc, l, kidx, c, bk, bkb):
            for (c0, ln, sg, _) in tc.segs:
                stt(x.t[:, c, c0:c0 + ln], bk[:, c0:c0 + ln], cf[:, l, 3 * kidx + 2, c, sg:sg + 1], x.t[:, c, c0:c0 + ln], ALU.mult, ALU.add,
                    [bkb, x.b[c], coef.b[0]], [x.b[c]])

        def ffn(tc, l, kidx):
            n = tc.T
            rms_mod(tc, l, kidx)
            for j in range(11):
                sl, parts = next_block(l, "ffn_in")
                for cc in range(2):
                    ba, bab = bank(0); bb, bbb = bank(1)
                    B.pe_group([(ba[:, 0:n], wap(sl, parts[0], kc, cc * 128, 128), h.t[:, kc, 0:n]) for kc in range(DC)], [sl.b[0]] + h.b, [bab])
                    B.pe_group([(bb[:, 0:n], wap(sl, parts[0], kc, 256 + cc * 128, 128), h.t[:, kc, 0:n]) for kc in range(DC)], [sl.b[0]] + h.b, [bbb])
                    ft, fb = fscr()
                    act(ft[:, 0:n], ba[:, 0:n], AF.Silu, [bab], [fb])
                    tt("dve", g.t[:, 2 * j + cc, 0:n], ft[:, 0:n], bb[:, 0:n], ALU.mult, [fb, bbb], [g.b[2 * j + cc]])
            for oc in range(8):
                sl, parts = next_block(l, "ffn_out")
                bo, bob = bank(oc % 2)
                B.pe_group([(bo[:, 0:n], wap(sl, parts[0], kc, 0, 128), g.t[:, kc, 0:n]) for kc in range(FC)], [sl.b[0]] + g.b, [bob])
                resid_update(tc, l, kidx, oc, bo, bob)

        def mixer(tc, l):
            n = tc.T
            rms_mod(tc, l, 1)
            qT_b = g.b[0:8]; on_b = g.b[8:16]; ya_b = g.b[16:20]
            for (c0, ln, sg, pos0) in tc.segs:
                zo = tc.zoff(sg)
                if tc.sample:
                    B.dma("pool", zc.t[:, :, zo:zo + CK - 1], sconv[l, sg - 1].rearrange("(c p) j -> p c j", p=128), (), zc.b)
                else:
                    B.op("pool", lambda zo=zo: nc.gpsimd.tensor_copy(out=zc.t[:, :, zo:zo + CK - 1], in_=haloc.t[:, l, :].rearrange("p (c j) -> p c j", c=4)), haloc.b, zc.b)
            for i in range(2):
                sl, parts = next_block(l, "conv_in")
                for cc in range(2):
                    c = 2 * i + cc
                    ba, bab = bank(0); bg_, bgb = bank(1)
                    B.pe_group([(ba[:, 0:n], wap(sl, parts[0], kc, (2 * cc) * 128, 128), h.t[:, kc, 0:n]) for kc in range(DC)], [sl.b[0]] + h.b, [bab])
                    B.pe_group([(bg_[:, 0:n], wap(sl, parts[0], kc, (2 * cc + 1) * 128, 128), h.t[:, kc, 0:n]) for kc in range(DC)], [sl.b[0]] + h.b, [bgb])
                    ft, fb = fscr()
                    act(ft[:, 0:n], bg_[:, 0:n], AF.Sigmoid, [bgb], [fb])
                    for (c0, ln, sg, pos0) in tc.segs:
                        zo = tc.zoff(sg) + CK - 1
                        tt("dve", zc.t[:, c, zo:zo + ln], ft[:, c0:c0 + ln], ba[:, c0:c0 + ln], ALU.mult, [fb, bab], [zc.b[c]])
            for (c0, ln, sg, pos0) in tc.segs:
                zo = tc.zoff(sg)
                if tc.sample:
                    out_dma(convso[l, sg - 1].rearrange("(c p) j -> p c j", p=128), zc.t[:, :, zo + ln:zo + ln + CK - 1], zc.b)
                else:
                    if tc.last:
                        out_dma(convo[l].rearrange("(c p) j -> p c j", p=128), zc.t[:, :, zo + ln:zo + ln + CK - 1], zc.b)
                    else:
                        B.op("pool", lambda zo=zo, ln=ln: nc.gpsimd.tensor_copy(out=haloc.t[:, l, :].rearrange("p (c j) -> p c j", c=4), in_=zc.t[:, :, zo + ln:zo + ln + CK - 1]), zc.b, haloc.b)
                for j in range(CK):
                    for c in range(4):
                        wj = vecs.t[:, l, V_WDW + c * CK + j:V_WDW + c * CK + j + 1]
                        if j == 0:
                            ts("dve", cacc.t[:, c, c0:c0 + ln], zc.t[:, c, zo:zo + ln], wj, vecs.t[:, l, V_BDW + c:V_BDW + c + 1], ALU.mult, ALU.add,
                               [zc.b[c], vecs.b[0]], [cacc.b[c]])
                        else:
                            stt(cacc.t[:, c, c0:c0 + ln], zc.t[:, c, zo + j:zo + j + ln], wj, cacc.t[:, c, c0:c0 + ln], ALU.mult, ALU.add,
                                [zc.b[c], cacc.b[c], vecs.b[0]], [cacc.b[c]])
            bm, bmb = bank(0)
            for c in range(4):
                B.op("pe", lambda c=c: nc.tensor.matmul(bm[:, 0:n], ones512.t[:, :], cacc.t[:, c, 0:n], start=(c == 0), stop=(c == 3)), [cacc.b[c], ones512.b[0]], [bmb])
            for c in range(4):
                tt("dve", cacc.t[:, c, 0:n], cacc.t[:, c, 0:n], bm[:, 0:n], ALU.subtract, [cacc.b[c], bmb], [cacc.b[c]])
            bv, bvb = bank(1)
            for c in range(4):
                ft, fb = fscr()
                act(ft[:, 0:n], cacc.t[:, c, 0:n], AF.Square, [cacc.b[c]], [fb])
                B.op("pe", lambda c=c, ft=ft: nc.tensor.matmul(bv[:, 0:n], ones512.t[:, :], ft[:, 0:n], start=(c == 0), stop=(c == 3)), [fb, ones512.b[0]], [bvb])
            rt, rb = rstd_from(bv, bvb, n)
            for c in range(4):
                ft, fb = fscr()
                tt("pool", ft[:, 0:n], cacc.t[:, c, 0:n], rt[:, 0:n], ALU.mult, [cacc.b[c], rb], [fb])
                act(g.t[:, 16 + c, 0:n], ft[:, 0:n], AF.Silu, [fb, vecs.b[0]], [g.b[16 + c]],
                    bias=vecs.t[:, l, V_LNB + c:V_LNB + c + 1], scale=vecs.t[:, l, V_LNG + c:V_LNG + c + 1])
            if cfg.get("DEBUG") == "conv":
                return
            for (c0, ln, sg, pos0) in tc.segs:
                po_ = tc.poff(sg)
                if tc.sample:
                    B.dma("pool", pc.t[:, :, po_:po_ + PB], spool[l, sg - 1].rearrange("(c p) j -> p c j", p=128), (), pc.b)
                else:
                    B.op("pool", lambda po_=po_: nc.gpsimd.tensor_copy(out=pc.t[:, :, po_:po_ + PB], in_=halop.t[:, l, :].rearrange("p (c j) -> p c j", c=4)), halop.b, pc.b)
            sl, parts = next_block(l, "pool_in")
            for c in range(4):
                bp, bpb = bank(c % 2)
                B.pe_group([(bp[:, 0:n], wap(sl, parts[0], kc, c * 128, 128), h.t[:, kc, 0:n]) for kc in range(DC)], [sl.b[0]] + h.b, [bpb])
                for (c0, ln, sg, pos0) in tc.segs:
                    po_ = tc.poff(sg) + PB
                    act(pc.t[:, c, po_:po_ + ln], bp[:, c0:c0 + ln], AF.Copy, [bpb], [pc.b[c]])
            for (c0, ln, sg, pos0) in tc.segs:
                po_ = tc.poff(sg)
                if tc.sample:
                    out_dma(poolso[l, sg - 1].rearrange("(c p) j -> p c j", p=128), pc.t[:, :, po_ + ln:po_ + ln + PB], pc.b)
                elif tc.last:
                    out_dma(poolo[l].rearrange("(c p) j -> p c j", p=128), pc.t[:, :, po_ + ln:po_ + ln + PB], pc.b)
                else:
                    B.op("pool", lambda po_=po_, ln=ln: nc.gpsimd.tensor_copy(out=halop.t[:, l, :].rearrange("p (c j) -> p c j", c=4), in_=pc.t[:, :, po_ + ln:po_ + ln + PB]), pc.b, halop.b)
                for gi, w in enumerate(POOL_WINDOWS):
                    cur, curb, lo = pc.t[:, gi, po_:po_ + PB + ln], pc.b[gi], 0
                    step = 1
                    while step < w:
                        nlo = PB - (w - 2 * step)
                        ft, fb = fscr()
                        cnt = PB + ln - nlo
                        tt("pool", ft[:, 0:cnt], cur[:, nlo - lo:nlo - lo + cnt], cur[:, nlo - lo - step:nlo - lo - step + cnt], ALU.add, [curb], [fb])
                        cur, curb, lo = ft, fb, nlo
                        step *= 2
                    pcur = pc.t[:, gi, po_ + PB:po_ + PB + ln]
                    stt(dpl.t[:, gi, c0:c0 + ln], cur[:, 0:ln], 1.0 / w, pcur, ALU.mult, ALU.subtract, [curb, pc.b[gi]], [dpl.b[gi]])
                    if pos0 == 0:
                        ft2, fb2 = fscr()
                        tt("dve", ft2[:, 0:16], cur[:, 0:16], invc.t[:, gi, :], ALU.mult, [curb, invc.b[0]], [fb2])
                        tt("dve", dpl.t[:, gi, c0:c0 + 16], ft2[:, 0:16], pcur[:, 0:16], ALU.subtract, [fb2, pc.b[gi]], [dpl.b[gi]])
            sl, parts = next_block(l, "pool_grp")
            for gi in range(4):
                bm2, bm2b = bank(gi % 2)
                B.pe_group([(bm2[:, 0:n], wap(sl, parts[gi], 0, 0, 128), dpl.t[:, gi, 0:n])], [sl.b[0], dpl.b[gi]], [bm2b])
                act(mpool.t[:, gi, 0:n], bm2[:, 0:n], AF.Copy, [bm2b, vecs.b[0]], [mpool.b[gi]], scale=vecs.t[:, l, V_PSC + gi:V_PSC + gi + 1])
            if cfg.get("DEBUG") == "pool":
                return
            for which in ("q_in", "k_in"):
                isq = which == "q_in"
                gvec = vecs.t[:, l, (V_GQ if isq else V_GK):(V_GQ if isq else V_GK) + 1]
                for i in range(2):
                    sl, parts = next_block(l, which)
                    for cc in range(4):
                        hh = 4 * i + cc
                        bq, bqb = bank(0)
                        B.pe_group([(bq[:, 0:n], wap(sl, parts[0], kc, cc * 128, 128), h.t[:, kc, 0:n]) for kc in range(DC)], [sl.b[0]] + h.b, [bqb])
                        ft, fb = fscr()
                        act(ft[:, 0:n], bq[:, 0:n], AF.Square, [bqb], [fb])
                        bs_, bsb = bank(1)
                        B.op("pe", lambda ft=ft, bs_=bs_: nc.tensor.matmul(bs_[:, 0:n], blk64.t[:, :], ft[:, 0:n], start=True, stop=True), [fb, blk64.b[0]], [bsb])
                        rt, rb = rstd_from(bs_, bsb, n)
                        qn, qnb = fscr()
                        stt(qn[:, 0:n], bq[:, 0:n], gvec, rt[:, 0:n], ALU.mult, ALU.mult, [bqb, rb, vecs.b[0]], [qnb])
                        br, brb = bank(1)
                        B.op("pe", lambda qn=qn, br=br: nc.tensor.matmul(br[:, 0:n], rotm.t[:, :], qn[:, 0:n], start=True, stop=True), [qnb, rotm.b[0]], [brb])
                        t1, t1b = fscr()
                        tt("pool", t1[:, 0:n], qn[:, 0:n], rope.t[:, 0, 0:n], ALU.mult, [qnb, rope.b[0]], [t1b])
                        t2, t2b = fscr()
                        tt("dve", t2[:, 0:n], br[:, 0:n], rope.t[:, 1, 0:n], ALU.mult, [brb, rope.b[0]], [t2b])
                        if isq:
                            tt("dve", g.t[:, hh, 0:n], t1[:, 0:n], t2[:, 0:n], ALU.add, [t1b, t2b], [g.b[hh]])
                        else:
                            tt("dve", t1[:, 0:n], t1[:, 0:n], t2[:, 0:n], ALU.add, [t1b, t2b], [t1b])
                            kb, kbb = bscr()
                            act(kb[:, 0:n], t1[:, 0:n], AF.Copy, [t1b], [kbb])
                            if tc.sample:
                                out_dma(ksTo[l, hh * 128:(hh + 1) * 128, :], t1[:, 0:n], [t1b])
                                B.dma("pool", kscr_s[l, hh, :, :], kb[:, 0:n], [kbb], [kscrs_b[l]])
                            else:
                                out_dma(kTo[l, hh * 128:(hh + 1) * 128, tc.t0:tc.t0 + n], t1[:, 0:n], [t1b])
                                B.dma("pool", kscr[l, hh, :, tc.t0:tc.t0 + n], kb[:, 0:n], [kbb], [kscr_b[l]])
            if cfg.get("DEBUG") == "qk":
                return
            ntb = (n + 127) // 128
            for half in range(2):
                sl, parts = next_block(l, "v_in")
                for tb in range(ntb):
                    nt_ = min(128, n - tb * 128)
                    bv2, bv2b = bank(tb % 2)
                    B.pe_group([(bv2[0:nt_, 0:512], h.t[:, kc, tb * 128:tb * 128 + nt_], wap(sl, parts[0], kc, 0, 512)) for kc in range(DC)], [sl.b[0]] + h.b, [bv2b])
                    act(vst.t[0:nt_, tb, :], bv2[0:nt_, 0:512], AF.Copy, [bv2b], vst.b)
                    B.op("pool", lambda nt_=nt_, tb=tb: nc.gpsimd.tensor_copy(out=vbf.t[0:nt_, tb, :], in_=vst.t[0:nt_, tb, :]), vst.b, vbf.b)
                cs = slice(half * 512, (half + 1) * 512)
                if cfg.get("VNODMA"):
                    continue
                if tc.sample:
                    out_dma(vso[l, :, cs], vst.t[0:n, 0, :], vst.b)
                    B.dma("pool", vscr_s[l, :, cs], vbf.t[0:n, 0, :], vbf.b, [vscrs_b[l]])
                else:
                    out_dma(vo[l, tc.t0:tc.t0 + n, cs].rearrange("(tb p) v -> p tb v", p=128), vst.t[:, 0:ntb, :], vst.b)
                    B.dma("pool", vscr[l, tc.t0:tc.t0 + n, cs].rearrange("(tb p) v -> p tb v", p=128), vbf.t[:, 0:ntb, :], vbf.b, [vscr_b[l]])
            if cfg.get("DEBUG") == "v":
                return
            for (c0, ln, sg, pos0) in tc.segs:
                for hh in range(NH):
                    attention_head(tc, l, hh, c0, ln, sg)
            if cfg.get("DEBUG") == "attn":
                return
            for c in range(8):
                sl, parts = next_block(l, "merge")
                sgs = []
                for b_ in range(3):
                    bg_, bgb = bank(b_ % 2)
                    B.pe_group([(bg_[:, 0:n], wap(sl, parts[b_], kc, 0, 128), h.t[:, kc, 0:n]) for kc in range(DC)], [sl.b[0]] + h.b, [bgb])
                    ft, fb = fscr()
                    act(ft[:, 0:n], bg_[:, 0:n], AF.Sigmoid, [bgb], [fb])
                    sgs.append((ft, fb))
                srcs = [(parts[3], 4, 16, ya_b), (parts[4], 4, None, mpool.b), (parts[5], 8, 8, on_b)]
                acc = None
                for b_, (part, kcn, goff, rb_) in enumerate(srcs):
                    by, byb = bank(b_ % 2)
                    if goff is None:
                        items = [(by[:, 0:n], wap(sl, part, kc, 0, 128), mpool.t[:, kc, 0:n]) for kc in range(kcn)]
                    else:
                        items = [(by[:, 0:n], wap(sl, part, kc, 0, 128), g.t[:, goff + kc, 0:n]) for kc in range(kcn)]
                    B.pe_group(items, [sl.b[0]] + list(rb_), [byb])
                    ft, fb = sgs[b_]
                    tt("dve", ft[:, 0:n], ft[:, 0:n], by[:, 0:n], ALU.mult, [fb, byb], [fb])
                tt("pool", sgs[0][0][:, 0:n], sgs[0][0][:, 0:n], sgs[1][0][:, 0:n], ALU.add, [sgs[0][1], sgs[1][1]], [sgs[0][1]])
                tt("dve", g.t[:, c, 0:n], sgs[0][0][:, 0:n], sgs[2][0][:, 0:n], ALU.add, [sgs[0][1], sgs[2][1]], [g.b[c]])
            for i in range(2):
                sl, parts = next_block(l, "w_out")
                for cc in range(4):
                    oc = 4 * i + cc
                    bo, bob = bank(oc % 2)
                    B.pe_group([(bo[:, 0:n], wap(sl, parts[0], kc, cc * 128, 128), g.t[:, kc, 0:n]) for kc in range(DC)], [sl.b[0]] + g.b[0:8], [bob])
                    resid_update(tc, l, 1, oc, bo, bob)

        ost = {"i": 0, "h": 0}
        sacc_t = [STile(B, f"sacc{i}", [128, TT], F32) for i in range(4)]
        ones1 = STile(B, "ones1", [128, 128], F32)
        mset("dve", ones1.t[:], 1.0, ones1.b)

        def attention_head(tc, l, hh, c0, ln, sg):
            if tc.sample:
                srcs = [("cache", PAST), ("new", tc.T)]
            else:
                srcs = [("scr", tc.t0 + ln)]
            pieces = []
            for kind, nkeys in srcs:
                if kind == "new":
                    pieces.append((kind, c0, ln))
                else:
                    for k0 in range(0, nkeys, KP):
                        pieces.append((kind, k0, min(KP, nkeys - k0)))
            hi = ost["h"]; ost["h"] += 1
            oacc = [bank(1) for _ in range(2)]
            sacc = [sacc_t[(hi % 2) * 2 + m] for m in range(2)]
            for m in range(2):
                mset("dve" if m == 0 else "pool", sacc[m].t[:, 0:ln], 0.0, sacc[m].b)
            units = []
            for pi, (kind, k0, nk) in enumerate(pieces):
                for kt in range((nk + 127) // 128):
                    nkk = min(128, nk - kt * 128)
                    kabs = k0 + kt * 128
                    qlo = 0; diag = False
                    if kind == "scr" and kabs >= tc.t0:
                        qlo = kabs - tc.t0; diag = True
                    units.append((pi, kt, nkk, qlo, diag))
            loaded = {}

            def load_piece(pi):
                kind, k0, nk = pieces[pi]
                i = ost["i"]; ost["i"] += 1
                kt_, vt_ = kTp[i % 2], vp[i % 2]
                nkt = (nk + 127) // 128
                if kind == "scr":
                    B.dma("sp", kt_.t[:, 0:nk], kscr[l, hh, :, k0:k0 + nk], [kscr_b[l]], kt_.b)
                    B.dma("sp", vt_.t[:, 0:nkt, :], vscr[l, k0:k0 + nk, hh * 128:(hh + 1) * 128].rearrange("(j p) v -> p j v", p=128), [vscr_b[l]], vt_.b)
                elif kind == "cache":
                    B.dma("pool", kt_.t[:, 0:nk], ckT[l, sg - 1, hh, :, k0:k0 + nk], (), kt_.b)
                    B.dma("pool", vt_.t[:, 0:nkt, :], cv[l, sg - 1, k0:k0 + nk, hh * 128:(hh + 1) * 128].rearrange("(j p) v -> p j v", p=128), (), vt_.b)
                else:
                    B.dma("sp", kt_.t[:, 0:nk], kscr_s[l, hh, :, k0:k0 + nk], [kscrs_b[l]], kt_.b)
                    B.dma("sp", vt_.t[0:nk, 0, :], vscr_s[l, k0:k0 + nk, hh * 128:(hh + 1) * 128], [vscrs_b[l]], vt_.b)
                loaded[pi] = (kt_, vt_)

            def emit_scores(ui):
                pi, kt, nkk, qlo, diag = units[ui]
                if pi not in loaded:
                    load_piece(pi)
                kt_, vt_ = loaded[pi]
                nq = ln - qlo
                res = []
                for m in range(2):
                    sc, scb = bank(0)
                    B.pe_group([(sc[0:nkk, 0:nq], kt_.t[m * 64:(m + 1) * 64, kt * 128:kt * 128 + nkk], g.t[m * 64:(m + 1) * 64, hh, c0 + qlo:c0 + ln])],
                               [kt_.b[0], g.b[hh]], [scb])
                    res.append((sc, scb))
                return res

            nu = len(units)
            nxt = emit_scores(0)
            for ui in range(nu):
                cur = nxt
                if ui + 1 < nu:
                    nxt = emit_scores(ui + 1)
                pi, kt, nkk, qlo, diag = units[ui]
                kt_, vt_ = loaded[pi]
                nq = ln - qlo
                for m in range(2):
                    sc, scb = cur[m]
                    pT, pTb = bscr()
                    act(pT[0:nkk, 0:nq], sc[0:nkk, 0:nq], AF.Exp, [scb], [pTb], scale=HD ** -0.5)
                    if diag:
                        mset("pool", pT[64:128, 0:64], 0.0, [pTb])
                    oa, oab = oacc[m]
                    B.op("pe", lambda: nc.tensor.matmul(oa[:, qlo:ln], vt_.t[0:nkk, kt, :], pT[0:nkk, 0:nq], start=(ui == 0), stop=(ui == nu - 1)), [vt_.b[0], pTb], [oab])
                    tt("dve" if m == 0 else "pool", sacc[m].t[0:nkk, qlo:ln], sacc[m].t[0:nkk, qlo:ln], pT[0:nkk, 0:nq], ALU.add, [sacc[m].b[0], pTb], sacc[m].b)
            os_ = []
            for m in range(2):
                oa, oab = oacc[m]
                sa, sab = bank(0)
                B.op("pe", lambda: nc.tensor.matmul(sa[:, 0:ln], ones1.t[:, :], sacc[m].t[:, 0:ln], start=True, stop=True), [sacc[m].b[0], ones1.b[0]], [sab])
                rt, rb = fscr()
                act(rt[:, 0:ln], sa[:, 0:ln], AF.Ln, [sab], [rb])
                act(rt[:, 0:ln], rt[:, 0:ln], AF.Exp, [rb], [rb], scale=-1.0)
                tt("dve", rt[:, 0:ln], rt[:, 0:ln], oa[:, 0:ln], ALU.mult, [rb, oab], [rb])
                os_.append((rt, rb))
            o1, o1b = os_[0]; o2, o2b = os_[1]
            stt(o1[:, 0:ln], o2[:, 0:ln], lams.t[:, 4 * l:4 * l + 1], o1[:, 0:ln], ALU.mult, ALU.add, [o1b, o2b, lams.b[0]], [o1b])
            act(o2[:, 0:ln], o1[:, 0:ln], AF.Square, [o1b], [o2b])
            bn, bnb = bank(0)
            B.op("pe", lambda: nc.tensor.matmul(bn[:, 0:ln], ones128.t[:, :], o2[:, 0:ln], start=True, stop=True), [o2b, ones128.b[0]], [bnb])
            rt, rb = rstd_from(bn, bnb, ln)
            tt("dve", o1[:, 0:ln], o1[:, 0:ln], rt[:, 0:ln], ALU.mult, [o1b, rb], [o1b])
            act(g.t[:, 8 + hh, c0:c0 + ln], o1[:, 0:ln], AF.Copy, [o1b, gsub.b[0]], [g.b[8 + hh]], scale=gsub.t[:, l:l + 1])

        def run_tile(tc):
            n = tc.T
            if tc.sample:
                B.dma("pool", x.t[:, :, 0:n], xsT.rearrange("(c p) s -> p c s", p=128), (), x.b)
                B.dma("pool", rope.t[:, :, 0:n], ropes_d[:, :, :], (), rope.b)
            else:
                B.dma("pool", x.t[:, :, 0:n], xT[:, tc.t0:tc.t0 + n].rearrange("(c p) s -> p c s", p=128), (), x.b)
                B.dma("pool", rope.t[:, :, 0:n], ropep_d[:, :, tc.t0:tc.t0 + n], (), rope.b)
            for l in range(L):
                assert wst["i"] == 0
                ffn(tc, l, 0)
                if cfg.get("DEBUG") == "ffn1":
                    return
                mixer(tc, l)
                if cfg.get("DEBUG") in ("mixer", "conv", "pool", "qk", "v", "attn"):
                    return
                ffn(tc, l, 2)
            if tc.sample:
                out_dma(ysT.rearrange("(c p) s -> p c s", p=128), x.t[:, :, 0:n], x.b)
            else:
                out_dma(yT[:, tc.t0:tc.t0 + n].rearrange("(c p) s -> p c s", p=128), x.t[:, :, 0:n], x.b)

        for t in range(NT):
            tc = TileCtx()
            tc.T = TT; tc.t0 = t * TT; tc.sample = False; tc.last = (t == NT - 1)
            tc.segs = [(0, TT, 0, t * TT)]
            tc.zoff = lambda sg: 0
            tc.poff = lambda sg: 0
            run_tile(tc)
        if cfg.get("DEBUG"):
            B.barrier("sp")
            return nc
        tc = TileCtx()
        tc.T = TS; tc.t0 = 0; tc.sample = True; tc.last = True
        tc.segs = [(0, DS, 1, PAST), (DS, DS, 2, PAST)]
        tc.zoff = lambda sg: (sg - 1) * (CK - 1 + DS)
        tc.poff = lambda sg: (sg - 1) * (PB + DS)
        run_tile(tc)
        B.barrier("sp")
        print("instructions emitted ~", B.nins, "sems", len(B.sems), flush=True)
    return nc


def _pack_inputs(inp, cfg):
    SEQ, L, PAST, DS = cfg["SEQ"], cfg["DEPTH"], cfg["PAST"], cfg["DEC_SEQ"]
    TS = 2 * DS
    f = np.float32
    W = {k: np.asarray(inp[k]) for k in ("w_ffn1_in", "w_ffn1_out", "w_in", "w_conv_out", "w_pool_out", "w_attn_out", "w_out", "w_ffn2_in", "w_ffn2_out")}
    W["w_pool_grp"] = np.asarray(inp["w_pool_grp"]).reshape(L, 4 * 128, 128)
    wflat = np.empty((L, 128, WPL), f)
    for l in range(L):
        o = 0
        for (_, parts) in BLOCKS:
            for (name, row0, KC, c) in parts:
                sub = W[name][l][row0:row0 + KC * 128][:, c]
                n = KC * len(c)
                wflat[l, :, o:o + n] = sub.reshape(KC, 128, len(c)).transpose(1, 0, 2).reshape(128, n)
                o += n
    wada = np.ascontiguousarray(np.asarray(inp["w_ada"]).reshape(L, DC, 128, 18, 512).transpose(0, 3, 2, 1, 4).reshape(L, 18, 128, 4096))

    def fm(v, nch):
        return np.asarray(v).reshape(L, nch, 128).transpose(2, 0, 1)
    vecs = np.zeros((128, L, NVEC), f)
    vecs[:, :, V_G1:V_G1 + 8] = fm(inp["g_ffn1"], 8); vecs[:, :, V_GM:V_GM + 8] = fm(inp["g_mix"], 8); vecs[:, :, V_G2:V_G2 + 8] = fm(inp["g_ffn2"], 8)
    vecs[:, :, V_BDW:V_BDW + 4] = fm(inp["b_dw"], 4); vecs[:, :, V_LNG:V_LNG + 4] = fm(inp["ln_conv_g"], 4); vecs[:, :, V_LNB:V_LNB + 4] = fm(inp["ln_conv_b"], 4)
    vecs[:, :, V_PSC:V_PSC + 4] = fm(inp["pool_scale"], 4)
    vecs[:, :, V_WDW:V_WDW + 124] = np.asarray(inp["w_dw"]).reshape(L, CK, 4, 128).transpose(3, 0, 2, 1).reshape(128, L, 124)
    vecs[:, :, V_GQ] = np.tile(np.asarray(inp["g_q"]), (1, 2)).T; vecs[:, :, V_GK] = np.tile(np.asarray(inp["g_k"]), (1, 2)).T
    vecs[:, :, V_GS] = np.asarray(inp["g_sub"]).T
    vecs[:, :, V_BADA:V_BADA + 72] = fm(inp["b_ada"], 72)
    lamv = np.stack([np.asarray(inp[k]) for k in ("lam_q1", "lam_k1", "lam_q2", "lam_k2")], axis=1)
    lamv = np.ascontiguousarray(np.broadcast_to(lamv[None], (128, L, 4, HD))).astype(f)
    rotm = np.zeros((128, 128), f)
    for p in range(128):
        if p % 64 < 32:
            rotm[p + 32, p] = -1.0
        else:
            rotm[p - 32, p] = 1.0
    half = HD // 2
    inv = (np.float32(10000.0) ** (-np.arange(half, dtype=np.float32) / np.float32(half))).astype(f)

    def rope_tab(pos):
        ang = pos.astype(f)[None, :] * inv[:, None]
        cs = np.stack([np.cos(ang), np.sin(ang)], axis=1).astype(f)
        return np.ascontiguousarray(np.tile(cs, (4, 1, 1)))
    ropep = rope_tab(np.arange(SEQ))
    rs = rope_tab(PAST + np.arange(DS))
    ropes = np.ascontiguousarray(np.concatenate([rs, rs], axis=2))
    xp = np.asarray(inp["x_prompt"]); xs = np.asarray(inp["x_sample"])
    ck = np.asarray(inp["cache_attn_k"]); cvv = np.asarray(inp["cache_attn_v"])
    sc = np.asarray(inp["state_conv"]); sp = np.asarray(inp["state_pool"])
    cp = np.asarray(inp["c_prompt"]); csm = np.asarray(inp["c_sample"])
    shared = dict(vecs=vecs, lamv=lamv, rotm=rotm, ropep=ropep, ropes=ropes, wada=wada, wflat=wflat)
    maps = []
    for i in range(8):
        b = i % cfg["BATCH"]
        s0 = 2 * i
        m = dict(shared)
        m["xT"] = np.ascontiguousarray(xp[b].T)
        m["xsT"] = np.ascontiguousarray(xs[s0:s0 + 2].reshape(TS, D).T)
        m["ckT"] = np.ascontiguousarray(ck[:, s0:s0 + 2].reshape(L, 2, PAST, NH, 128).transpose(0, 1, 3, 4, 2))
        m["cv"] = np.ascontiguousarray(cvv[:, s0:s0 + 2].reshape(L, 2, PAST, D))
        m["sconv"] = np.ascontiguousarray(sc[:, s0:s0 + 2].transpose(0, 1, 3, 2))
        m["spool"] = np.ascontiguousarray(sp[:, s0:s0 + 2].transpose(0, 1, 3, 2))
        cc = np.stack([cp[b], csm[s0], csm[s0 + 1]], axis=0)
        m["cT"] = np.ascontiguousarray(cc.reshape(3, DC, 128).transpose(2, 1, 0))
        maps.append(m)
    return maps


def _unpack(res, cfg):
    SEQ, L, PAST, DS = cfg["SEQ"], cfg["DEPTH"], cfg["PAST"], cfg["DEC_SEQ"]
    NB, NDB = cfg["BATCH"], cfg["DEC_BATCH"]
    f = np.float32
    r = res
    y_p = np.stack([np.asarray(r[b]["yT"]).T for b in range(NB)]).astype(f)
    k_p = np.stack([np.asarray(r[b]["kTo"]).transpose(0, 2, 1) for b in range(NB)], axis=1).reshape(L, NB, SEQ, NH, 2, HD).astype(f)
    v_p = np.stack([np.asarray(r[b]["vo"]) for b in range(NB)], axis=1).reshape(L, NB, SEQ, NH, 128).astype(f)
    c_p = np.stack([np.asarray(r[b]["convo"]).transpose(0, 2, 1) for b in range(NB)], axis=1).astype(f)
    p_p = np.stack([np.asarray(r[b]["poolo"]).transpose(0, 2, 1) for b in range(NB)], axis=1).astype(f)
    y_s = np.concatenate([np.asarray(r[i]["ysT"]).T.reshape(2, DS, D) for i in range(8)], axis=0).astype(f)
    k_s = np.concatenate([np.asarray(r[i]["ksTo"]).transpose(0, 2, 1).reshape(L, 2, DS, NH, 2, HD) for i in range(8)], axis=1).astype(f)
    v_s = np.concatenate([np.asarray(r[i]["vso"]).reshape(L, 2, DS, NH, 128) for i in range(8)], axis=1).astype(f)
    c_s = np.concatenate([np.asarray(r[i]["convso"]).transpose(0, 1, 3, 2) for i in range(8)], axis=1).astype(f)
    p_s = np.concatenate([np.asarray(r[i]["poolso"]).transpose(0, 1, 3, 2) for i in range(8)], axis=1).astype(f)
    return (np.ascontiguousarray(y_p), np.ascontiguousarray(y_s), np.ascontiguousarray(k_p), np.ascontiguousarray(v_p),
            np.ascontiguousarray(c_p), np.ascontiguousarray(p_p), np.ascontiguousarray(k_s), np.ascontiguousarray(v_s),
            np.ascontiguousarray(c_s), np.ascontiguousarray(p_s))


def kernel(**inputs):
    cfg = dict(CFG)
    nc = build_program(cfg)
    maps = _pack_inputs(inputs, cfg)
    res = run_bass_kernel_spmd(nc, maps, core_ids=list(range(8)))
    return _unpack(res.results, cfg)
```

```python
import contextlib
import math
import numpy as np
import concourse.bass as bass
import concourse.mybir as mybir
from concourse.bass_utils import run_bass_kernel_spmd

F32 = mybir.dt.float32
BF16 = mybir.dt.bfloat16
AF = mybir.ActivationFunctionType
ALU = mybir.AluOpType

CFG = dict(SEQ=8192, DEPTH=4, PAST=4096, BATCH=4, DEC_BATCH=16, DEC_SEQ=16)
D = 1024; DC = 8; DFF = 2816; FC = 22; NH = 8; HD = 64
CW = 512; CK = 31; PW = 512; PB = 15; EPS = 1e-6
POOL_WINDOWS = (2, 4, 8, 16)
TT = 512
KP = 1024
SLOT = 5120
NSLOT = 4
NVEC = 24 + 16 + 124 + 3 + 72
V_G1, V_GM, V_G2, V_BDW, V_LNG, V_LNB, V_PSC, V_WDW, V_GQ, V_GK, V_GS, V_BADA = 0, 8, 16, 24, 28, 32, 36, 40, 164, 165, 166, 167
SEM_ROT = 30000


def layer_blocks():
    blocks = []

    def cols(s, n):
        return np.arange(s, s + n)

    def ffn_in(name):
        for j in range(11):
            c = np.concatenate([cols(256 * j, 128), cols(256 * j + 128, 128), cols(DFF + 256 * j, 128), cols(DFF + 256 * j + 128, 128)])
            blocks.append(("ffn_in", [(name, 0, 8, c)]))

    def ffn_out(name):
        for j in range(8):
            blocks.append(("ffn_out", [(name, 0, 22, cols(128 * j, 128))]))

    ffn_in("w_ffn1_in"); ffn_out("w_ffn1_out")
    for i in range(2):
        c = np.concatenate([cols(256 * i, 128), cols(512 + 256 * i, 128), cols(256 * i + 128, 128), cols(512 + 256 * i + 128, 128)])
        blocks.append(("conv_in", [("w_in", 0, 8, c)]))
    blocks.append(("pool_in", [("w_in", 0, 8, cols(1024, 512))]))
    blocks.append(("pool_grp", [("w_pool_grp", g * 128, 1, cols(0, 128)) for g in range(4)]))
    for i in range(2):
        blocks.append(("q_in", [("w_in", 0, 8, cols(1536 + 512 * i, 512))]))
    for i in range(2):
        blocks.append(("k_in", [("w_in", 0, 8, cols(2560 + 512 * i, 512))]))
    for i in range(2):
        blocks.append(("v_in", [("w_in", 0, 8, cols(3584 + 512 * i, 512))]))
    for c in range(8):
        blocks.append(("merge", [("w_in", 0, 8, cols(4608 + 128 * c, 128)), ("w_in", 0, 8, cols(5632 + 128 * c, 128)),
                                 ("w_in", 0, 8, cols(6656 + 128 * c, 128)), ("w_conv_out", 0, 4, cols(128 * c, 128)),
                                 ("w_pool_out", 0, 4, cols(128 * c, 128)), ("w_attn_out", 0, 8, cols(128 * c, 128))]))
    for i in range(2):
        blocks.append(("w_out", [("w_out", 0, 8, cols(512 * i, 512))]))
    ffn_in("w_ffn2_in"); ffn_out("w_ffn2_out")
    return blocks


BLOCKS = layer_blocks()
BLK_SZ = [sum(kc * len(c) for (_, _, kc, c) in parts) for (_, parts) in BLOCKS]
BLK_OFF = [0] + list(np.cumsum(BLK_SZ))
WPL = int(BLK_OFF[-1])
NPIECE = 10


def piece_of_block():
    nb = len(BLOCKS)
    per = (nb + NPIECE - 1) // NPIECE
    return [min(b // per, NPIECE - 1) for b in range(nb)], per


class Tok:
    __slots__ = ("si", "val")

    def __init__(self, si, val):
        self.si = si; self.val = val


class Buf:
    __slots__ = ("w", "r")

    def __init__(self):
        self.w = None; self.r = {}


class Builder:
    def __init__(self, nc, es):
        self.nc = nc; self.es = es
        self.E = {"pe": nc.tensor, "act": nc.scalar, "dve": nc.vector, "pool": nc.gpsimd, "sp": nc.sync}
        self.sems = []; self.owner = []; self.final = []
        self.est = {}
        self.waited = {e: {} for e in self.E}
        for e in ("pe", "act", "dve", "pool"):
            self.est[e] = [self.new_sem(e), 0]
        self.dsems = {}
        for q, n in (("sp", 8), ("pool", 8)):
            self.dsems[q] = [[self.new_sem("dma"), 0] for _ in range(n)]
        self.drr = {"sp": 0, "pool": 0}
        self.nins = 0

    def new_sem(self, owner):
        s = self.es.enter_context(self.nc.semaphore(f"s{len(self.sems)}"))
        self.sems.append(s); self.owner.append(owner); self.final.append(0)
        return len(self.sems) - 1

    def need(self, e, tok, raw=True):
        if tok is None:
            return
        self.need_sv(e, tok.si, tok.val, raw)

    def need_sv(self, e, si, val, raw):
        ow = self.owner[si]
        if ow == e and e == "pe":
            return
        d = self.waited[e]
        if d.get(si, 0) >= val:
            return
        self.E[e].wait_ge(self.sems[si], val)
        d[si] = val
        self.nins += 1

    def mark(self, e, ins):
        st = self.est[e]
        if st[1] >= SEM_ROT:
            st[0] = self.new_sem(e); st[1] = 0
        st[1] += 1
        ins.then_inc(self.sems[st[0]], 1)
        self.final[st[0]] = st[1]
        self.nins += 1
        return Tok(st[0], st[1])

    def hazards(self, e, reads, writes):
        for b in reads:
            self.need(e, b.w, True)
        for b in writes:
            self.need(e, b.w, False)
            for si, val in b.r.items():
                self.need_sv(e, si, val, False)

    def commit(self, tok, reads, writes):
        for b in reads:
            if b.r.get(tok.si, 0) < tok.val:
                b.r[tok.si] = tok.val
        for b in writes:
            b.w = tok; b.r = {}

    def op(self, e, fn, reads=(), writes=()):
        self.hazards(e, reads, writes)
        tok = self.mark(e, fn())
        self.commit(tok, reads, writes)
        return tok

    def pe_group(self, items, reads, writes):
        self.hazards("pe", reads, writes)
        n = len(items)
        ins = None
        for i, (o, l, r) in enumerate(items):
            ins = self.nc.tensor.matmul(o, l, r, start=(i == 0), stop=(i == n - 1))
        self.nins += n - 1
        tok = self.mark("pe", ins)
        self.commit(tok, reads, writes)
        return tok

    def dma(self, q, out, in_, reads=(), writes=()):
        pool = self.dsems[q]
        i = self.drr[q]; self.drr[q] = (i + 1) % len(pool)
        si, cnt = pool[i]
        if cnt > 0:
            self.need_sv(q, si, cnt, True)
        self.hazards(q, reads, writes)
        self.E[q].dma_start(out=out, in_=in_).then_inc(self.sems[si], 16)
        pool[i][1] = cnt + 16
        self.final[si] = cnt + 16
        tok = Tok(si, cnt + 16)
        self.commit(tok, reads, writes)
        self.nins += 1
        return tok

    def barrier(self, e):
        for si, v in enumerate(self.final):
            if v > 0:
                self.need_sv(e, si, v, True)


class STile:
    def __init__(self, B, name, shape, dtype, nbuf=None):
        self.t = B.es.enter_context(B.nc.sbuf_tensor("sb_" + name, list(shape), dtype))
        n = nbuf if nbuf is not None else (shape[1] if len(shape) == 3 else 1)
        self.b = [Buf() for _ in range(n)]


def build_program(cfg):
    SEQ, L, PAST, DS = cfg["SEQ"], cfg["DEPTH"], cfg["PAST"], cfg["DEC_SEQ"]
    NT = SEQ // TT
    TS = 2 * DS
    nc = bass.Bass("TRN2", target_bir_lowering=False)
    dt = nc.dram_tensor

    def din(name, shape, dtype=F32):
        return dt(name, list(shape), dtype, kind="ExternalInput").ap()

    def dout(name, shape, dtype=F32):
        return dt(name, list(shape), dtype, kind="ExternalOutput").ap()

    xT = din("xT", [D, SEQ]); xsT = din("xsT", [D, TS])
    ckT = din("ckT", [L, 2, NH, 128, PAST]); cv = din("cv", [L, 2, PAST, D])
    sconv = din("sconv", [L, 2, CW, CK - 1]); spool = din("spool", [L, 2, PW, PB])
    cT_d = din("cT", [128, DC, 3]); vecs_d = din("vecs", [128, L, NVEC]); lam_d = din("lamv", [128, L, 4, HD])
    rotm_d = din("rotm", [128, 128]); ropep_d = din("ropep", [128, 2, SEQ]); ropes_d = din("ropes", [128, 2, TS])
    wada_d = din("wada", [L, 18, 128, 4096]); wflat_d = din("wflat", [L, 128, WPL])
    yT = dout("yT", [D, SEQ]); ysT = dout("ysT", [D, TS])
    kTo = dout("kTo", [L, D, SEQ]); vo = dout("vo", [L, SEQ, D]); convo = dout("convo", [L, CW, CK - 1]); poolo = dout("poolo", [L, PW, PB])
    ksTo = dout("ksTo", [L, D, TS]); vso = dout("vso", [L, TS, D]); convso = dout("convso", [L, 2, CW, CK - 1]); poolso = dout("poolso", [L, 2, PW, PB])
    wsc = dt("wsc", [L, 128, WPL], BF16, kind="Internal").ap()
    kscr = dt("kscr", [L, NH, 128, SEQ], BF16, kind="Internal").ap()
    vscr = dt("vscr", [L, SEQ, D], BF16, kind="Internal").ap()
    kscr_s = dt("kscrs", [L, NH, 128, TS], BF16, kind="Internal").ap()
    vscr_s = dt("vscrs", [L, TS, D], BF16, kind="Internal").ap()

    es = contextlib.ExitStack()
    with es:
        B = Builder(nc, es)
        E = B.E
        x = STile(B, "x", [128, DC, TT], F32)
        h = STile(B, "h", [128, DC, TT], BF16)
        g = STile(B, "g", [128, FC, TT], BF16)
        wring = [STile(B, f"wr{i}", [128, SLOT], BF16) for i in range(NSLOT)]
        NF = 6
        fs = [STile(B, f"fs{i}", [128, 544], F32) for i in range(NF)]
        NBS = 6
        bs = [STile(B, f"bs{i}", [128, TT], BF16) for i in range(NBS)]
        zc = STile(B, "zc", [128, 4, TT + 32], F32)
        cacc = STile(B, "cacc", [128, 4, TT], F32)
        pc = STile(B, "pc", [128, 4, TT + 16], F32)
        dpl = STile(B, "dpl", [128, 4, TT], BF16)
        mpool = STile(B, "mpool", [128, 4, TT], BF16)
        vst = STile(B, "vst", [128, 4, 512], F32, nbuf=1)
        vbf = STile(B, "vbf", [128, 4, 512], BF16, nbuf=1)
        rope = STile(B, "rope", [128, 2, TT], F32, nbuf=1)
        kTp = [STile(B, f"kTp{i}", [128, KP], BF16) for i in range(2)]
        vp = [STile(B, f"vp{i}", [128, KP // 128, 128], BF16, nbuf=1) for i in range(2)]
        onesD = STile(B, "onesD", [128, 128], F32); ones512 = STile(B, "ones512", [128, 128], F32)
        ones128 = STile(B, "ones128", [128, 128], F32); blk64 = STile(B, "blk64", [128, 128], F32)
        rotm = STile(B, "rotm", [128, 128], F32); onesb = STile(B, "onesb", [128, 128], BF16)
        vecs = STile(B, "vecs", [128, L, NVEC], F32, nbuf=1)
        coef = STile(B, "coef", [128, L * 9 * DC * 3], F32)
        modT = STile(B, "modT", [128, L, 72 * 3], F32, nbuf=1)
        lamt = STile(B, "lamt", [128, L, 4, HD], F32, nbuf=1)
        lams = STile(B, "lams", [128, 16 * L], F32)
        gsub = STile(B, "gsubs", [128, L], F32)
        cTt = STile(B, "cTt", [128, DC, 3], F32, nbuf=1)
        siluc = STile(B, "siluc", [128, DC, 4], BF16, nbuf=1)
        haloc = STile(B, "haloc", [128, L, 4 * (CK - 1)], F32)
        halop = STile(B, "halop", [128, L, 4 * PB], F32)
        invc = STile(B, "invc", [128, 4, 16], F32, nbuf=1)
        banks = [B.es.enter_context(nc.psum_tensor(f"pb{i}", [128, 512], F32)) for i in range(8)]
        bbuf = [Buf() for _ in range(8)]
        st = {"ring": [0, 0], "f": 0, "b": 0, "w": 0, "rs": 0}

        def bank(r=0):
            i = st["ring"][r]; st["ring"][r] = (i + 1) % 4
            k = r * 4 + i
            return banks[k], bbuf[k]

        def fscr():
            i = st["f"]; st["f"] = (i + 1) % NF
            return fs[i].t, fs[i].b[0]

        def bscr():
            i = st["b"]; st["b"] = (i + 1) % NBS
            return bs[i].t, bs[i].b[0]

        def act(out, in_, func, reads, writes, bias=0.0, scale=1.0):
            return B.op("act", lambda: nc.scalar.activation(out=out, in_=in_, func=func, bias=bias, scale=scale), reads, writes)

        def tt(e, out, in0, in1, op, reads, writes):
            return B.op(e, lambda: E[e].tensor_tensor(out=out, in0=in0, in1=in1, op=op), reads, writes)

        def ts(e, out, in0, s1, s2, op0, op1, reads, writes):
            if s2 is None:
                return B.op(e, lambda: E[e].tensor_scalar(out=out, in0=in0, scalar1=s1, scalar2=None, op0=op0), reads, writes)
            return B.op(e, lambda: E[e].tensor_scalar(out=out, in0=in0, scalar1=s1, scalar2=s2, op0=op0, op1=op1), reads, writes)

        def stt(out, in0, s, in1, op0, op1, reads, writes):
            return B.op("dve", lambda: nc.vector.scalar_tensor_tensor(out=out, in0=in0, scalar=s, in1=in1, op0=op0, op1=op1), reads, writes)

        def mset(e, ap, val, writes):
            return B.op(e, lambda: E[e].memset(ap, val), (), writes)

        rsr = [STile(B, f"rsr{i}", [128, TT], F32) for i in range(2)]

        def rstd_from(bk, bkb, n, eps=EPS):
            i = st["rs"]; st["rs"] = (i + 1) % 2
            ft, fb = rsr[i].t, rsr[i].b[0]
            act(ft[:, 0:n], bk[:, 0:n], AF.Ln, [bkb], [fb], bias=epsb.t[:, 0:1])
            act(ft[:, 0:n], ft[:, 0:n], AF.Exp, [fb], [fb], scale=-0.5)
            return ft, fb

        epsb = STile(B, "epsb", [128, 1], F32)

        mset("dve", epsb.t[:], EPS, epsb.b)
        mset("dve", onesD.t[:], 1.0 / D, onesD.b)
        mset("dve", ones512.t[:], 1.0 / CW, ones512.b)
        mset("dve", ones128.t[:], 1.0 / 128, ones128.b)
        mset("dve", blk64.t[:], 0.0, blk64.b)
        mset("dve", blk64.t[0:64, 0:64], 1.0 / HD, blk64.b)
        mset("dve", blk64.t[64:128, 64:128], 1.0 / HD, blk64.b)
        mset("pool", onesb.t[:], 1.0, onesb.b)
        mset("pool", haloc.t[:], 0.0, haloc.b)
        mset("pool", halop.t[:], 0.0, halop.b)
        for gi, w in enumerate(POOL_WINDOWS):
            mset("pool", invc.t[:, gi, :], 1.0 / w, invc.b)
            for t_ in range(w - 1):
                mset("pool", invc.t[:, gi, t_:t_ + 1], 1.0 / (t_ + 1), invc.b)
        B.dma("sp", rotm.t[:], rotm_d[:, :], (), rotm.b)
        B.dma("sp", vecs.t[:], vecs_d[:, :, :], (), vecs.b)
        B.dma("sp", lamt.t[:], lam_d[:, :, :, :], (), lamt.b)
        B.dma("sp", cTt.t[:], cT_d[:, :, :], (), cTt.b)

        for l in range(L):
            lam_init = 0.8 - 0.6 * math.exp(-0.3 * l)
            ft, fb = fscr()
            for j in range(2):
                tt("dve", ft[:, j * 64:(j + 1) * 64], lamt.t[:, l, 2 * j, :], lamt.t[:, l, 2 * j + 1, :], ALU.mult, lamt.b, [fb])
                B.op("dve", lambda j=j: nc.vector.reduce_sum(out=lams.t[:, 4 * l + 1 + j:4 * l + 2 + j], in_=ft[:, j * 64:(j + 1) * 64], axis=mybir.AxisListType.X), [fb], lams.b)
                act(lams.t[:, 4 * l + 1 + j:4 * l + 2 + j], lams.t[:, 4 * l + 1 + j:4 * l + 2 + j], AF.Exp, lams.b, lams.b)
            tt("dve", lams.t[:, 4 * l + 3:4 * l + 4], lams.t[:, 4 * l + 2:4 * l + 3], lams.t[:, 4 * l + 1:4 * l + 2], ALU.subtract, lams.b, lams.b)
            ts("dve", lams.t[:, 4 * l:4 * l + 1], lams.t[:, 4 * l + 3:4 * l + 4], -lam_init, None, ALU.add, None, lams.b, lams.b)
            ts("dve", gsub.t[:, l:l + 1], vecs.t[:, l, V_GS:V_GS + 1], 1.0 - lam_init, None, ALU.mult, None, vecs.b, gsub.b)

        if cfg.get("DEBUG") == "consts":
            B.barrier("sp")
            return nc
        act(siluc.t[:, :, 0:3], cTt.t[:, :, :], AF.Silu, cTt.b, siluc.b)
        for l in range(L):
            for bi in range(18):
                sl = wring[st["w"] % NSLOT]; st["w"] += 1
                B.dma("pool", sl.t[:, 0:4096], wada_d[l, bi, :, :], (), sl.b)
                bk, bkb = bank(0)
                for j in range(4):
                    items = [(bk[:, 3 * j:3 * j + 3], sl.t[:, kc * 512 + j * 128:kc * 512 + (j + 1) * 128], siluc.t[:, kc, 0:3]) for kc in range(DC)]
                    B.pe_group(items, [sl.b[0], siluc.b[0]], [bkb])
                B.op("dve", lambda bk=bk, bi=bi, l=l: nc.vector.tensor_copy(out=modT.t[:, l, bi * 12:(bi + 1) * 12], in_=bk[:, 0:12]), [bkb], modT.b)
            mv = modT.t[:, l, :].rearrange("p (c s) -> p c s", s=3)
            for s in range(3):
                tt("dve", mv[:, :, s], mv[:, :, s], vecs.t[:, l, V_BADA:V_BADA + 72], ALU.add, modT.b + vecs.b, modT.b)
        cf = coef.t[:, :].rearrange("p (l k c s) -> p l k c s", l=L, k=9, c=DC)
        for l in range(L):
            mv = modT.t[:, l, :].rearrange("p (k c s) -> p k c s", k=9, c=DC)
            for i, (gv, gmul) in enumerate(((V_G1, 0.5), (V_GM, 1.0), (V_G2, 0.5))):
                for s in range(3):
                    stt(cf[:, l, 3 * i, :, s], mv[:, 3 * i + 1, :, s], 1.0, vecs.t[:, l, gv:gv + DC], ALU.add, ALU.mult, modT.b + vecs.b, coef.b)
                    B.op("dve", lambda l=l, i=i, s=s, mv=mv: nc.vector.tensor_copy(out=cf[:, l, 3 * i + 1, :, s], in_=mv[:, 3 * i, :, s]), modT.b, coef.b)
                    ts("dve", cf[:, l, 3 * i + 2, :, s], mv[:, 3 * i + 2, :, s], gmul, None, ALU.mult, None, modT.b, coef.b)

        if cfg.get("DEBUG") == "mod":
            B.barrier("sp")
            return nc
        pob, per = piece_of_block()
        wpiece = [[Buf() for _ in range(NPIECE)] for _ in range(L)]
        for l in range(L):
            for p in range(NPIECE):
                b0, b1 = p * per, min((p + 1) * per, len(BLOCKS))
                if b0 >= b1:
                    continue
                a, b_ = int(BLK_OFF[b0]), int(BLK_OFF[b1])
                B.dma("pool", wsc[l, :, a:b_], wflat_d[l, :, a:b_], (), [wpiece[l][p]])

        if cfg.get("DEBUG") == "wcast":
            B.barrier("sp")
            return nc
        wst = {"l": 0, "i": 0}

        def next_block(l, label):
            i = wst["i"]
            assert BLOCKS[i][0] == label, (BLOCKS[i][0], label)
            wst["i"] = (i + 1) % len(BLOCKS)
            sl = wring[st["w"] % NSLOT]; st["w"] += 1
            off, sz = int(BLK_OFF[i]), int(BLK_SZ[i])
            B.dma("sp", sl.t[:, 0:sz], wsc[l, :, off:off + sz], [wpiece[l][pob[i]]], sl.b)
            parts = []
            po = 0
            for (_, _, kc, c) in BLOCKS[i][1]:
                parts.append((po, kc, len(c)))
                po += kc * len(c)
            return sl, parts

        def wap(sl, part, kc, c0, n):
            po, KC, ncol = part
            a = po + kc * ncol + c0
            return sl.t[:, a:a + n]

        kscr_b = [Buf() for _ in range(L)]; vscr_b = [Buf() for _ in range(L)]
        kscrs_b = [Buf() for _ in range(L)]; vscrs_b = [Buf() for _ in range(L)]
        outb = Buf()

        class TileCtx:
            pass

        def out_dma(out, in_, reads):
            ob = Buf()
            return B.dma("pool", out, in_, reads, [ob])

        def rms_mod(tc, l, kidx):
            n = tc.T
            bk, bkb = bank(0)
            for c in range(DC):
                ft, fb = fscr()
                act(ft[:, 0:n], x.t[:, c, 0:n], AF.Square, [x.b[c]], [fb])
                B.op("pe", lambda c=c, ft=ft: nc.tensor.matmul(bk[:, 0:n], onesD.t[:, :], ft[:, 0:n], start=(c == 0), stop=(c == DC - 1)), [fb, onesD.b[0]], [bkb])
            rt, rb = rstd_from(bk, bkb, n)
            for c in range(DC):
                ft, fb = fscr()
                tt("pool" if c % 2 else "dve", ft[:, 0:n], x.t[:, c, 0:n], rt[:, 0:n], ALU.mult, [x.b[c], rb], [fb])
                for (c0, ln, sg, _) in tc.segs:
                    act(h.t[:, c, c0:c0 + ln], ft[:, c0:c0 + ln], AF.Identity, [fb, coef.b[0]], [h.b[c]],
                        bias=cf[:, l, 3 * kidx + 1, c, sg:sg + 1], scale=cf[:, l, 3 * kidx, c, sg:sg + 1])

        def resid_update(tc, l, kidx, c, bk, bkb):
            for (c0, ln, sg, _) in tc.segs:
                stt(x.t[:, c, c0:c0 + ln], bk[:, c0:c0 + ln], cf[:, l, 3 * kidx + 2, c, sg:sg + 1], x.t[:, c, c0:c0 + ln], ALU.mult, ALU.add,
                    [bkb, x.b[c], coef.b[0]], [x.b[c]])

        def ffn(tc, l, kidx):
            n = tc.T
            rms_mod(tc, l, kidx)
            for j in range(11):
                sl, parts = next_block(l, "ffn_in")
                for cc in range(2):
                    ba, bab = bank(0); bb, bbb = bank(1)
                    B.pe_group([(ba[:, 0:n], wap(sl, parts[0], kc, cc * 128, 128), h.t[:, kc, 0:n]) for kc in range(DC)], [sl.b[0]] + h.b, [bab])
                    B.pe_group([(bb[:, 0:n], wap(sl, parts[0], kc, 256 + cc * 128, 128), h.t[:, kc, 0:n]) for kc in range(DC)], [sl.b[0]] + h.b, [bbb])
                    ft, fb = fscr()
                    act(ft[:, 0:n], ba[:, 0:n], AF.Silu, [bab], [fb])
                    tt("dve", g.t[:, 2 * j + cc, 0:n], ft[:, 0:n], bb[:, 0:n], ALU.mult, [fb, bbb], [g.b[2 * j + cc]])
            for oc in range(8):
                sl, parts = next_block(l, "ffn_out")
                bo, bob = bank(oc % 2)
                B.pe_group([(bo[:, 0:n], wap(sl, parts[0], kc, 0, 128), g.t[:, kc, 0:n]) for kc in range(FC)], [sl.b[0]] + g.b, [bob])
                resid_update(tc, l, kidx, oc, bo, bob)

        def mixer(tc, l):
            n = tc.T
            rms_mod(tc, l, 1)
            qT_b = g.b[0:8]; on_b = g.b[8:16]; ya_b = g.b[16:20]
            for (c0, ln, sg, pos0) in tc.segs:
                zo = tc.zoff(sg)
                if tc.sample:
                    B.dma("pool", zc.t[:, :, zo:zo + CK - 1], sconv[l, sg - 1].rearrange("(c p) j -> p c j", p=128), (), zc.b)
                else:
                    B.op("pool", lambda zo=zo: nc.gpsimd.tensor_copy(out=zc.t[:, :, zo:zo + CK - 1], in_=haloc.t[:, l, :].rearrange("p (c j) -> p c j", c=4)), haloc.b, zc.b)
            for i in range(2):
                sl, parts = next_block(l, "conv_in")
                for cc in range(2):
                    c = 2 * i + cc
                    ba, bab = bank(0); bg_, bgb = bank(1)
                    B.pe_group([(ba[:, 0:n], wap(sl, parts[0], kc, (2 * cc) * 128, 128), h.t[:, kc, 0:n]) for kc in range(DC)], [sl.b[0]] + h.b, [bab])
                    B.pe_group([(bg_[:, 0:n], wap(sl, parts[0], kc, (2 * cc + 1) * 128, 128), h.t[:, kc, 0:n]) for kc in range(DC)], [sl.b[0]] + h.b, [bgb])
                    ft, fb = fscr()
                    act(ft[:, 0:n], bg_[:, 0:n], AF.Sigmoid, [bgb], [fb])
                    for (c0, ln, sg, pos0) in tc.segs:
                        zo = tc.zoff(sg) + CK - 1
                        tt("dve", zc.t[:, c, zo:zo + ln], ft[:, c0:c0 + ln], ba[:, c0:c0 + ln], ALU.mult, [fb, bab], [zc.b[c]])
            for (c0, ln, sg, pos0) in tc.segs:
                zo = tc.zoff(sg)
                if tc.sample:
                    out_dma(convso[l, sg - 1].rearrange("(c p) j -> p c j", p=128), zc.t[:, :, zo + ln:zo + ln + CK - 1], zc.b)
                else:
                    if tc.last:
                        out_dma(convo[l].rearrange("(c p) j -> p c j", p=128), zc.t[:, :, zo + ln:zo + ln + CK - 1], zc.b)
                    else:
                        B.op("pool", lambda zo=zo, ln=ln: nc.gpsimd.tensor_copy(out=haloc.t[:, l, :].rearrange("p (c j) -> p c j", c=4), in_=zc.t[:, :, zo + ln:zo + ln + CK - 1]), zc.b, haloc.b)
                for j in range(CK):
                    for c in range(4):
                        wj = vecs.t[:, l, V_WDW + c * CK + j:V_WDW + c * CK + j + 1]
                        if j == 0:
                            ts("dve", cacc.t[:, c, c0:c0 + ln], zc.t[:, c, zo:zo + ln], wj, vecs.t[:, l, V_BDW + c:V_BDW + c + 1], ALU.mult, ALU.add,
                               [zc.b[c], vecs.b[0]], [cacc.b[c]])
                        else:
                            stt(cacc.t[:, c, c0:c0 + ln], zc.t[:, c, zo + j:zo + j + ln], wj, cacc.t[:, c, c0:c0 + ln], ALU.mult, ALU.add,
                                [zc.b[c], cacc.b[c], vecs.b[0]], [cacc.b[c]])
            bm, bmb = bank(0)
            for c in range(4):
                B.op("pe", lambda c=c: nc.tensor.matmul(bm[:, 0:n], ones512.t[:, :], cacc.t[:, c, 0:n], start=(c == 0), stop=(c == 3)), [cacc.b[c], ones512.b[0]], [bmb])
            for c in range(4):
                tt("dve", cacc.t[:, c, 0:n], cacc.t[:, c, 0:n], bm[:, 0:n], ALU.subtract, [cacc.b[c], bmb], [cacc.b[c]])
            bv, bvb = bank(1)
            for c in range(4):
                ft, fb = fscr()
                act(ft[:, 0:n], cacc.t[:, c, 0:n], AF.Square, [cacc.b[c]], [fb])
                B.op("pe", lambda c=c, ft=ft: nc.tensor.matmul(bv[:, 0:n], ones512.t[:, :], ft[:, 0:n], start=(c == 0), stop=(c == 3)), [fb, ones512.b[0]], [bvb])
            rt, rb = rstd_from(bv, bvb, n)
            for c in range(4):
                ft, fb = fscr()
                tt("pool", ft[:, 0:n], cacc.t[:, c, 0:n], rt[:, 0:n], ALU.mult, [cacc.b[c], rb], [fb])
                act(g.t[:, 16 + c, 0:n], ft[:, 0:n], AF.Silu, [fb, vecs.b[0]], [g.b[16 + c]],
                    bias=vecs.t[:, l, V_LNB + c:V_LNB + c + 1], scale=vecs.t[:, l, V_LNG + c:V_LNG + c + 1])
            if cfg.get("DEBUG") == "conv":
                return
            for (c0, ln, sg, pos0) in tc.segs:
                po_ = tc.poff(sg)
                if tc.sample:
                    B.dma("pool", pc.t[:, :, po_:po_ + PB], spool[l, sg - 1].rearrange("(c p) j -> p c j", p=128), (), pc.b)
                else:
                    B.op("pool", lambda po_=po_: nc.gpsimd.tensor_copy(out=pc.t[:, :, po_:po_ + PB], in_=halop.t[:, l, :].rearrange("p (c j) -> p c j", c=4)), halop.b, pc.b)
            sl, parts = next_block(l, "pool_in")
            for c in range(4):
                bp, bpb = bank(c % 2)
                B.pe_group([(bp[:, 0:n], wap(sl, parts[0], kc, c * 128, 128), h.t[:, kc, 0:n]) for kc in range(DC)], [sl.b[0]] + h.b, [bpb])
                for (c0, ln, sg, pos0) in tc.segs:
                    po_ = tc.poff(sg) + PB
                    act(pc.t[:, c, po_:po_ + ln], bp[:, c0:c0 + ln], AF.Copy, [bpb], [pc.b[c]])
            for (c0, ln, sg, pos0) in tc.segs:
                po_ = tc.poff(sg)
                if tc.sample:
                    out_dma(poolso[l, sg - 1].rearrange("(c p) j -> p c j", p=128), pc.t[:, :, po_ + ln:po_ + ln + PB], pc.b)
                elif tc.last:
                    out_dma(poolo[l].rearrange("(c p) j -> p c j", p=128), pc.t[:, :, po_ + ln:po_ + ln + PB], pc.b)
                else:
                    B.op("pool", lambda po_=po_, ln=ln: nc.gpsimd.tensor_copy(out=halop.t[:, l, :].rearrange("p (c j) -> p c j", c=4), in_=pc.t[:, :, po_ + ln:po_ + ln + PB]), pc.b, halop.b)
                for gi, w in enumerate(POOL_WINDOWS):
                    cur, curb, lo = pc.t[:, gi, po_:po_ + PB + ln], pc.b[gi], 0
                    step = 1
                    while step < w:
                        nlo = PB - (w - 2 * step)
                        ft, fb = fscr()
                        cnt = PB + ln - nlo
                        tt("pool", ft[:, 0:cnt], cur[:, nlo - lo:nlo - lo + cnt], cur[:, nlo - lo - step:nlo - lo - step + cnt], ALU.add, [curb], [fb])
                        cur, curb, lo = ft, fb, nlo
                        step *= 2
                    pcur = pc.t[:, gi, po_ + PB:po_ + PB + ln]
                    stt(dpl.t[:, gi, c0:c0 + ln], cur[:, 0:ln], 1.0 / w, pcur, ALU.mult, ALU.subtract, [curb, pc.b[gi]], [dpl.b[gi]])
                    if pos0 == 0:
                        ft2, fb2 = fscr()
                        tt("dve", ft2[:, 0:16], cur[:, 0:16], invc.t[:, gi, :], ALU.mult, [curb, invc.b[0]], [fb2])
                        tt("dve", dpl.t[:, gi, c0:c0 + 16], ft2[:, 0:16], pcur[:, 0:16], ALU.subtract, [fb2, pc.b[gi]], [dpl.b[gi]])
            sl, parts = next_block(l, "pool_grp")
            for gi in range(4):
                bm2, bm2b = bank(gi % 2)
                B.pe_group([(bm2[:, 0:n], wap(sl, parts[gi], 0, 0, 128), dpl.t[:, gi, 0:n])], [sl.b[0], dpl.b[gi]], [bm2b])
                act(mpool.t[:, gi, 0:n], bm2[:, 0:n], AF.Copy, [bm2b, vecs.b[0]], [mpool.b[gi]], scale=vecs.t[:, l, V_PSC + gi:V_PSC + gi + 1])
            if cfg.get("DEBUG") == "pool":
                return
            for which in ("q_in", "k_in"):
                isq = which == "q_in"
                gvec = vecs.t[:, l, (V_GQ if isq else V_GK):(V_GQ if isq else V_GK) + 1]
                for i in range(2):
                    sl, parts = next_block(l, which)
                    for cc in range(4):
                        hh = 4 * i + cc
                        bq, bqb = bank(0)
                        B.pe_group([(bq[:, 0:n], wap(sl, parts[0], kc, cc * 128, 128), h.t[:, kc, 0:n]) for kc in range(DC)], [sl.b[0]] + h.b, [bqb])
                        ft, fb = fscr()
                        act(ft[:, 0:n], bq[:, 0:n], AF.Square, [bqb], [fb])
                        bs_, bsb = bank(1)
                        B.op("pe", lambda ft=ft, bs_=bs_: nc.tensor.matmul(bs_[:, 0:n], blk64.t[:, :], ft[:, 0:n], start=True, stop=True), [fb, blk64.b[0]], [bsb])
                        rt, rb = rstd_from(bs_, bsb, n)
                        qn, qnb = fscr()
                        stt(qn[:, 0:n], bq[:, 0:n], gvec, rt[:, 0:n], ALU.mult, ALU.mult, [bqb, rb, vecs.b[0]], [qnb])
                        br, brb = bank(1)
                        B.op("pe", lambda qn=qn, br=br: nc.tensor.matmul(br[:, 0:n], rotm.t[:, :], qn[:, 0:n], start=True, stop=True), [qnb, rotm.b[0]], [brb])
                        t1, t1b = fscr()
                        tt("pool", t1[:, 0:n], qn[:, 0:n], rope.t[:, 0, 0:n], ALU.mult, [qnb, rope.b[0]], [t1b])
                        t2, t2b = fscr()
                        tt("dve", t2[:, 0:n], br[:, 0:n], rope.t[:, 1, 0:n], ALU.mult, [brb, rope.b[0]], [t2b])
                        if isq:
                            tt("dve", g.t[:, hh, 0:n], t1[:, 0:n], t2[:, 0:n], ALU.add, [t1b, t2b], [g.b[hh]])
                        else:
                            tt("dve", t1[:, 0:n], t1[:, 0:n], t2[:, 0:n], ALU.add, [t1b, t2b], [t1b])
                            kb, kbb = bscr()
                            act(kb[:, 0:n], t1[:, 0:n], AF.Copy, [t1b], [kbb])
                            if tc.sample:
                                out_dma(ksTo[l, hh * 128:(hh + 1) * 128, :], t1[:, 0:n], [t1b])
                                B.dma("pool", kscr_s[l, hh, :, :], kb[:, 0:n], [kbb], [kscrs_b[l]])
                            else:
                                out_dma(kTo[l, hh * 128:(hh + 1) * 128, tc.t0:tc.t0 + n], t1[:, 0:n], [t1b])
                                B.dma("pool", kscr[l, hh, :, tc.t0:tc.t0 + n], kb[:, 0:n], [kbb], [kscr_b[l]])
            if cfg.get("DEBUG") == "qk":
                return
            ntb = (n + 127) // 128
            for half in range(2):
                sl, parts = next_block(l, "v_in")
                for tb in range(ntb):
                    nt_ = min(128, n - tb * 128)
                    bv2, bv2b = bank(tb % 2)
                    B.pe_group([(bv2[0:nt_, 0:512], h.t[:, kc, tb * 128:tb * 128 + nt_], wap(sl, parts[0], kc, 0, 512)) for kc in range(DC)], [sl.b[0]] + h.b, [bv2b])
                    act(vst.t[0:nt_, tb, :], bv2[0:nt_, 0:512], AF.Copy, [bv2b], vst.b)
                    B.op("pool", lambda nt_=nt_, tb=tb: nc.gpsimd.tensor_copy(out=vbf.t[0:nt_, tb, :], in_=vst.t[0:nt_, tb, :]), vst.b, vbf.b)
                cs = slice(half * 512, (half + 1) * 512)
                if cfg.get("VNODMA"):
                    continue
                if tc.sample:
                    out_dma(vso[l, :, cs], vst.t[0:n, 0, :], vst.b)
                    B.dma("pool", vscr_s[l, :, cs], vbf.t[0:n, 0, :], vbf.b, [vscrs_b[l]])
                else:
                    out_dma(vo[l, tc.t0:tc.t0 + n, cs].rearrange("(tb p) v -> p tb v", p=128), vst.t[:, 0:ntb, :], vst.b)
                    B.dma("pool", vscr[l, tc.t0:tc.t0 + n, cs].rearrange("(tb p) v -> p tb v", p=128), vbf.t[:, 0:ntb, :], vbf.b, [vscr_b[l]])
            if cfg.get("DEBUG") == "v":
                return
            for (c0, ln, sg, pos0) in tc.segs:
                for hh in range(NH):
                    attention_head(tc, l, hh, c0, ln, sg)
            if cfg.get("DEBUG") == "attn":
                return
            for c in range(8):
                sl, parts = next_block(l, "merge")
                sgs = []
                for b_ in range(3):
                    bg_, bgb = bank(b_ % 2)
                    B.pe_group([(bg_[:, 0:n], wap(sl, parts[b_], kc, 0, 128), h.t[:, kc, 0:n]) for kc in range(DC)], [sl.b[0]] + h.b, [bgb])
                    ft, fb = fscr()
                    act(ft[:, 0:n], bg_[:, 0:n], AF.Sigmoid, [bgb], [fb])
                    sgs.append((ft, fb))
                srcs = [(parts[3], 4, 16, ya_b), (parts[4], 4, None, mpool.b), (parts[5], 8, 8, on_b)]
                acc = None
                for b_, (part, kcn, goff, rb_) in enumerate(srcs):
                    by, byb = bank(b_ % 2)
                    if goff is None:
                        items = [(by[:, 0:n], wap(sl, part, kc, 0, 128), mpool.t[:, kc, 0:n]) for kc in range(kcn)]
                    else:
                        items = [(by[:, 0:n], wap(sl, part, kc, 0, 128), g.t[:, goff + kc, 0:n]) for kc in range(kcn)]
                    B.pe_group(items, [sl.b[0]] + list(rb_), [byb])
                    ft, fb = sgs[b_]
                    tt("dve", ft[:, 0:n], ft[:, 0:n], by[:, 0:n], ALU.mult, [fb, byb], [fb])
                tt("pool", sgs[0][0][:, 0:n], sgs[0][0][:, 0:n], sgs[1][0][:, 0:n], ALU.add, [sgs[0][1], sgs[1][1]], [sgs[0][1]])
                tt("dve", g.t[:, c, 0:n], sgs[0][0][:, 0:n], sgs[2][0][:, 0:n], ALU.add, [sgs[0][1], sgs[2][1]], [g.b[c]])
            for i in range(2):
                sl, parts = next_block(l, "w_out")
                for cc in range(4):
                    oc = 4 * i + cc
                    bo, bob = bank(oc % 2)
                    B.pe_group([(bo[:, 0:n], wap(sl, parts[0], kc, cc * 128, 128), g.t[:, kc, 0:n]) for kc in range(DC)], [sl.b[0]] + g.b[0:8], [bob])
                    resid_update(tc, l, 1, oc, bo, bob)

        ost = {"i": 0, "h": 0}
        sacc_t = [STile(B, f"sacc{i}", [128, TT], F32) for i in range(4)]
        ones1 = STile(B, "ones1", [128, 128], F32)
        mset("dve", ones1.t[:], 1.0, ones1.b)

        def attention_head(tc, l, hh, c0, ln, sg):
            if tc.sample:
                srcs = [("cache", PAST), ("new", tc.T)]
            else:
                srcs = [("scr", tc.t0 + ln)]
            pieces = []
            for kind, nkeys in srcs:
                if kind == "new":
                    pieces.append((kind, c0, ln))
                else:
                    for k0 in range(0, nkeys, KP):
                        pieces.append((kind, k0, min(KP, nkeys - k0)))
            hi = ost["h"]; ost["h"] += 1
            oacc = [bank(1) for _ in range(2)]
            s1bank = bank(1)
            sacc = [sacc_t[(hi % 2) * 2 + j] for j in range(2)]
            for j in range(2):
                mset("pool", sacc[j].t[:, 0:ln], 0.0, sacc[j].b)
            units = []
            for pi, (kind, k0, nk) in enumerate(pieces):
                for kt in range((nk + 127) // 128):
                    nkk = min(128, nk - kt * 128)
                    kabs = k0 + kt * 128
                    qlo = 0; diag = False
                    if kind == "scr" and kabs >= tc.t0:
                        qlo = kabs - tc.t0; diag = True
                    units.append((pi, kt, nkk, qlo, diag))
            loaded = {}

            def load_piece(pi):
                kind, k0, nk = pieces[pi]
                i = ost["i"]; ost["i"] += 1
                kt_, vt_ = kTp[i % 2], vp[i % 2]
                nkt = (nk + 127) // 128
                if kind == "scr":
                    B.dma("sp", kt_.t[:, 0:nk], kscr[l, hh, :, k0:k0 + nk], [kscr_b[l]], kt_.b)
                    B.dma("sp", vt_.t[:, 0:nkt, :], vscr[l, k0:k0 + nk, hh * 128:(hh + 1) * 128].rearrange("(j p) v -> p j v", p=128), [vscr_b[l]], vt_.b)
                elif kind == "cache":
                    B.dma("pool", kt_.t[:, 0:nk], ckT[l, sg - 1, hh, :, k0:k0 + nk], (), kt_.b)
                    B.dma("pool", vt_.t[:, 0:nkt, :], cv[l, sg - 1, k0:k0 + nk, hh * 128:(hh + 1) * 128].rearrange("(j p) v -> p j v", p=128), (), vt_.b)
                else:
                    B.dma("sp", kt_.t[:, 0:nk], kscr_s[l, hh, :, k0:k0 + nk], [kscrs_b[l]], kt_.b)
                    B.dma("sp", vt_.t[0:nk, 0, :], vscr_s[l, k0:k0 + nk, hh * 128:(hh + 1) * 128], [vscrs_b[l]], vt_.b)
                loaded[pi] = (kt_, vt_)

            def emit_scores(ui):
                pi, kt, nkk, qlo, diag = units[ui]
                if pi not in loaded:
                    load_piece(pi)
                kt_, vt_ = loaded[pi]
                nq = ln - qlo
                res = []
                for m in range(2):
                    sc, scb = bank(0)
                    B.pe_group([(sc[0:nkk, 0:nq], kt_.t[m * 64:(m + 1) * 64, kt * 128:kt * 128 + nkk], g.t[m * 64:(m + 1) * 64, hh, c0 + qlo:c0 + ln])],
                               [kt_.b[0], g.b[hh]], [scb])
                    res.append((sc, scb))
                return res

            nu = len(units)
            nxt = emit_scores(0)
            for ui in range(nu):
                cur = nxt
                if ui + 1 < nu:
                    nxt = emit_scores(ui + 1)
                pi, kt, nkk, qlo, diag = units[ui]
                kt_, vt_ = loaded[pi]
                nq = ln - qlo
                for m in range(2):
                    sc, scb = cur[m]
                    pT, pTb = bscr()
                    act(pT[0:nkk, 0:nq], sc[0:nkk, 0:nq], AF.Exp, [scb], [pTb], scale=HD ** -0.5)
                    if diag:
                        mset("pool", pT[64:128, 0:64], 0.0, [pTb])
                    oa, oab = oacc[m]
                    B.op("pe", lambda: nc.tensor.matmul(oa[:, qlo:ln], vt_.t[0:nkk, kt, :], pT[0:nkk, 0:nq], start=(ui == 0), stop=(ui == nu - 1)), [vt_.b[0], pTb], [oab])
                    if m == 0:
                        sj = sacc[ui % 2]
                        tt("dve", sj.t[0:nkk, qlo:ln], sj.t[0:nkk, qlo:ln], pT[0:nkk, 0:nq], ALU.add, [sj.b[0], pTb], sj.b)
                    else:
                        B.op("pe", lambda: nc.tensor.matmul(s1bank[0][:, qlo:ln], onesb.t[0:nkk, :], pT[0:nkk, 0:nq], start=(ui == 0), stop=(ui == nu - 1)), [onesb.b[0], pTb], [s1bank[1]])
            os_ = []
            for m in range(2):
                oa, oab = oacc[m]
                if m == 0:
                    sa, sab = bank(0)
                    for j in range(2):
                        B.op("pe", lambda: nc.tensor.matmul(sa[:, 0:ln], ones1.t[:, :], sacc[j].t[:, 0:ln], start=(j == 0), stop=(j == 1)), [sacc[j].b[0], ones1.b[0]], [sab])
                else:
                    sa, sab = s1bank
                rt, rb = fscr()
                act(rt[:, 0:ln], sa[:, 0:ln], AF.Ln, [sab], [rb])
                act(rt[:, 0:ln], rt[:, 0:ln], AF.Exp, [rb], [rb], scale=-1.0)
                tt("dve", rt[:, 0:ln], rt[:, 0:ln], oa[:, 0:ln], ALU.mult, [rb, oab], [rb])
                os_.append((rt, rb))
            o1, o1b = os_[0]; o2, o2b = os_[1]
            stt(o1[:, 0:ln], o2[:, 0:ln], lams.t[:, 4 * l:4 * l + 1], o1[:, 0:ln], ALU.mult, ALU.add, [o1b, o2b, lams.b[0]], [o1b])
            act(o2[:, 0:ln], o1[:, 0:ln], AF.Square, [o1b], [o2b])
            bn, bnb = bank(0)
            B.op("pe", lambda: nc.tensor.matmul(bn[:, 0:ln], ones128.t[:, :], o2[:, 0:ln], start=True, stop=True), [o2b, ones128.b[0]], [bnb])
            rt, rb = rstd_from(bn, bnb, ln)
            tt("dve", o1[:, 0:ln], o1[:, 0:ln], rt[:, 0:ln], ALU.mult, [o1b, rb], [o1b])
            act(g.t[:, 8 + hh, c0:c0 + ln], o1[:, 0:ln], AF.Copy, [o1b, gsub.b[0]], [g.b[8 + hh]], scale=gsub.t[:, l:l + 1])

        def run_tile(tc):
            n = tc.T
            if tc.sample:
                B.dma("pool", x.t[:, :, 0:n], xsT.rearrange("(c p) s -> p c s", p=128), (), x.b)
                B.dma("pool", rope.t[:, :, 0:n], ropes_d[:, :, :], (), rope.b)
            else:
                B.dma("pool", x.t[:, :, 0:n], xT[:, tc.t0:tc.t0 + n].rearrange("(c p) s -> p c s", p=128), (), x.b)
                B.dma("pool", rope.t[:, :, 0:n], ropep_d[:, :, tc.t0:tc.t0 + n], (), rope.b)
            for l in range(L):
                assert wst["i"] == 0
                ffn(tc, l, 0)
                if cfg.get("DEBUG") == "ffn1":
                    return
                mixer(tc, l)
                if cfg.get("DEBUG") in ("mixer", "conv", "pool", "qk", "v", "attn"):
                    return
                ffn(tc, l, 2)
            if tc.sample:
                out_dma(ysT.rearrange("(c p) s -> p c s", p=128), x.t[:, :, 0:n], x.b)
            else:
                out_dma(yT[:, tc.t0:tc.t0 + n].rearrange("(c p) s -> p c s", p=128), x.t[:, :, 0:n], x.b)

        for t in range(NT):
            tc = TileCtx()
            tc.T = TT; tc.t0 = t * TT; tc.sample = False; tc.last = (t == NT - 1)
            tc.segs = [(0, TT, 0, t * TT)]
            tc.zoff = lambda sg: 0
            tc.poff = lambda sg: 0
            run_tile(tc)
        if cfg.get("DEBUG"):
            B.barrier("sp")
            return nc
        tc = TileCtx()
        tc.T = TS; tc.t0 = 0; tc.sample = True; tc.last = True
        tc.segs = [(0, DS, 1, PAST), (DS, DS, 2, PAST)]
        tc.zoff = lambda sg: (sg - 1) * (CK - 1 + DS)
        tc.poff = lambda sg: (sg - 1) * (PB + DS)
        run_tile(tc)
        B.barrier("sp")
        print("instructions emitted ~", B.nins, "sems", len(B.sems), flush=True)
    return nc


def _pack_inputs(inp, cfg):
    SEQ, L, PAST, DS = cfg["SEQ"], cfg["DEPTH"], cfg["PAST"], cfg["DEC_SEQ"]
    TS = 2 * DS
    f = np.float32
    W = {k: np.asarray(inp[k]) for k in ("w_ffn1_in", "w_ffn1_out", "w_in", "w_conv_out", "w_pool_out", "w_attn_out", "w_out", "w_ffn2_in", "w_ffn2_out")}
    W["w_pool_grp"] = np.asarray(inp["w_pool_grp"]).reshape(L, 4 * 128, 128)
    wflat = np.empty((L, 128, WPL), f)
    for l in range(L):
        o = 0
        for (_, parts) in BLOCKS:
            for (name, row0, KC, c) in parts:
                sub = W[name][l][row0:row0 + KC * 128][:, c]
                n = KC * len(c)
                wflat[l, :, o:o + n] = sub.reshape(KC, 128, len(c)).transpose(1, 0, 2).reshape(128, n)
                o += n
    wada = np.ascontiguousarray(np.asarray(inp["w_ada"]).reshape(L, DC, 128, 18, 512).transpose(0, 3, 2, 1, 4).reshape(L, 18, 128, 4096))

    def fm(v, nch):
        return np.asarray(v).reshape(L, nch, 128).transpose(2, 0, 1)
    vecs = np.zeros((128, L, NVEC), f)
    vecs[:, :, V_G1:V_G1 + 8] = fm(inp["g_ffn1"], 8); vecs[:, :, V_GM:V_GM + 8] = fm(inp["g_mix"], 8); vecs[:, :, V_G2:V_G2 + 8] = fm(inp["g_ffn2"], 8)
    vecs[:, :, V_BDW:V_BDW + 4] = fm(inp["b_dw"], 4); vecs[:, :, V_LNG:V_LNG + 4] = fm(inp["ln_conv_g"], 4); vecs[:, :, V_LNB:V_LNB + 4] = fm(inp["ln_conv_b"], 4)
    vecs[:, :, V_PSC:V_PSC + 4] = fm(inp["pool_scale"], 4)
    vecs[:, :, V_WDW:V_WDW + 124] = np.asarray(inp["w_dw"]).reshape(L, CK, 4, 128).transpose(3, 0, 2, 1).reshape(128, L, 124)
    vecs[:, :, V_GQ] = np.tile(np.asarray(inp["g_q"]), (1, 2)).T; vecs[:, :, V_GK] = np.tile(np.asarray(inp["g_k"]), (1, 2)).T
    vecs[:, :, V_GS] = np.asarray(inp["g_sub"]).T
    vecs[:, :, V_BADA:V_BADA + 72] = fm(inp["b_ada"], 72)
    lamv = np.stack([np.asarray(inp[k]) for k in ("lam_q1", "lam_k1", "lam_q2", "lam_k2")], axis=1)
    lamv = np.ascontiguousarray(np.broadcast_to(lamv[None], (128, L, 4, HD))).astype(f)
    rotm = np.zeros((128, 128), f)
    for p in range(128):
        if p % 64 < 32:
            rotm[p + 32, p] = -1.0
        else:
            rotm[p - 32, p] = 1.0
    half = HD // 2
    inv = (np.float32(10000.0) ** (-np.arange(half, dtype=np.float32) / np.float32(half))).astype(f)

    def rope_tab(pos):
        ang = pos.astype(f)[None, :] * inv[:, None]
        cs = np.stack([np.cos(ang), np.sin(ang)], axis=1).astype(f)
        return np.ascontiguousarray(np.tile(cs, (4, 1, 1)))
    ropep = rope_tab(np.arange(SEQ))
    rs = rope_tab(PAST + np.arange(DS))
    ropes = np.ascontiguousarray(np.concatenate([rs, rs], axis=2))
    xp = np.asarray(inp["x_prompt"]); xs = np.asarray(inp["x_sample"])
    ck = np.asarray(inp["cache_attn_k"]); cvv = np.asarray(inp["cache_attn_v"])
    sc = np.asarray(inp["state_conv"]); sp = np.asarray(inp["state_pool"])
    cp = np.asarray(inp["c_prompt"]); csm = np.asarray(inp["c_sample"])
    shared = dict(vecs=vecs, lamv=lamv, rotm=rotm, ropep=ropep, ropes=ropes, wada=wada, wflat=wflat)
    maps = []
    for i in range(8):
        b = i % cfg["BATCH"]
        s0 = 2 * i
        m = dict(shared)
        m["xT"] = np.ascontiguousarray(xp[b].T)
        m["xsT"] = np.ascontiguousarray(xs[s0:s0 + 2].reshape(TS, D).T)
        m["ckT"] = np.ascontiguousarray(ck[:, s0:s0 + 2].reshape(L, 2, PAST, NH, 128).transpose(0, 1, 3, 4, 2))
        m["cv"] = np.ascontiguousarray(cvv[:, s0:s0 + 2].reshape(L, 2, PAST, D))
        m["sconv"] = np.ascontiguousarray(sc[:, s0:s0 + 2].transpose(0, 1, 3, 2))
        m["spool"] = np.ascontiguousarray(sp[:, s0:s0 + 2].transpose(0, 1, 3, 2))
        cc = np.stack([cp[b], csm[s0], csm[s0 + 1]], axis=0)
        m["cT"] = np.ascontiguousarray(cc.reshape(3, DC, 128).transpose(2, 1, 0))
        maps.append(m)
    return maps


def _unpack(res, cfg):
    SEQ, L, PAST, DS = cfg["SEQ"], cfg["DEPTH"], cfg["PAST"], cfg["DEC_SEQ"]
    NB, NDB = cfg["BATCH"], cfg["DEC_BATCH"]
    f = np.float32
    r = res
    y_p = np.stack([np.asarray(r[b]["yT"]).T for b in range(NB)]).astype(f)
    k_p = np.stack([np.asarray(r[b]["kTo"]).transpose(0, 2, 1) for b in range(NB)], axis=1).reshape(L, NB, SEQ, NH, 2, HD).astype(f)
    v_p = np.stack([np.asarray(r[b]["vo"]) for b in range(NB)], axis=1).reshape(L, NB, SEQ, NH, 128).astype(f)
    c_p = np.stack([np.asarray(r[b]["convo"]).transpose(0, 2, 1) for b in range(NB)], axis=1).astype(f)
    p_p = np.stack([np.asarray(r[b]["poolo"]).transpose(0, 2, 1) for b in range(NB)], axis=1).astype(f)
    y_s = np.concatenate([np.asarray(r[i]["ysT"]).T.reshape(2, DS, D) for i in range(8)], axis=0).astype(f)
    k_s = np.concatenate([np.asarray(r[i]["ksTo"]).transpose(0, 2, 1).reshape(L, 2, DS, NH, 2, HD) for i in range(8)], axis=1).astype(f)
    v_s = np.concatenate([np.asarray(r[i]["vso"]).reshape(L, 2, DS, NH, 128) for i in range(8)], axis=1).astype(f)
    c_s = np.concatenate([np.asarray(r[i]["convso"]).transpose(0, 1, 3, 2) for i in range(8)], axis=1).astype(f)
    p_s = np.concatenate([np.asarray(r[i]["poolso"]).transpose(0, 1, 3, 2) for i in range(8)], axis=1).astype(f)
    return (np.ascontiguousarray(y_p), np.ascontiguousarray(y_s), np.ascontiguousarray(k_p), np.ascontiguousarray(v_p),
            np.ascontiguousarray(c_p), np.ascontiguousarray(p_p), np.ascontiguousarray(k_s), np.ascontiguousarray(v_s),
            np.ascontiguousarray(c_s), np.ascontiguousarray(p_s))


def kernel(**inputs):
    cfg = dict(CFG)
    nc = build_program(cfg)
    maps = _pack_inputs(inputs, cfg)
    res = run_bass_kernel_spmd(nc, maps, core_ids=list(range(8)))
    return _unpack(res.results, cfg)
```
